# Optimizing a Trainium2 kernel written in Bass

```python
import math
import jax, jax.numpy as jnp
from jax import lax
import numpy as np

D_MODEL = 2048
BATCH = 16
SEQ = 2048
DEPTH = 4

N_MIXERS = 4
D_FF = 4 * D_MODEL
NORM_EPS = 1e-6
FNET_GROUPS = 8
FNET_GC = D_MODEL // FNET_GROUPS
CONV_WIDTH = 31
GRID_W = 64
NA_HEADS = 16
NA_HEAD_DIM = D_MODEL // NA_HEADS
NA_WIN_H = 8
NA_WIN_W = 16
NA_QCOLS = 16
NA_KBAND = 2 * NA_WIN_W
S5_GC = 16
S5_GROUPS = D_MODEL // S5_GC
S5_STATE = 64

kernel_name = "hybrid_fnet_conformer_natten_s5_encoder"


def _rms_norm(x, g):
    xf = x.astype(jnp.float32)
    y = xf * lax.rsqrt(jnp.mean(jnp.square(xf), axis=-1, keepdims=True) + NORM_EPS)
    return (y * g.astype(jnp.float32)).astype(x.dtype)


def _layer_norm(x, g, b):
    xf = x.astype(jnp.float32)
    mu = jnp.mean(xf, axis=-1, keepdims=True)
    xc = xf - mu
    y = xc * lax.rsqrt(jnp.mean(jnp.square(xc), axis=-1, keepdims=True) + NORM_EPS)
    return (y * g.astype(jnp.float32) + b.astype(jnp.float32)).astype(x.dtype)


def _fourier_mixer(h, w, b):
    B_, S_, D_ = h.shape
    hf = h.astype(jnp.float32).reshape(B_, S_, FNET_GROUPS, FNET_GC)
    z = jnp.fft.fft2(hf, axes=(1, 3), norm="ortho").real
    z = z.reshape(B_, S_, D_).astype(h.dtype)
    return z @ w + b


def _conv_module(h, w_in, b_in, w_dw, b_dw, ln_g, ln_b, w_out, b_out):
    a, gate = jnp.split(h @ w_in + b_in, 2, axis=-1)
    u = a * jax.nn.sigmoid(gate)
    pad = CONV_WIDTH // 2
    u = lax.conv_general_dilated(
        u, w_dw[:, None, :].astype(u.dtype), window_strides=(1,),
        padding=[(pad, pad)], dimension_numbers=("NWC", "WIO", "NWC"),
        feature_group_count=u.shape[-1]) + b_dw
    u = jax.nn.silu(_layer_norm(u, ln_g, ln_b))
    return u @ w_out + b_out


def _na_column_tables():
    ncb = GRID_W // NA_QCOLS
    qcol = np.arange(GRID_W)
    cstart = np.clip(qcol - NA_WIN_W // 2, 0, GRID_W - NA_WIN_W)
    kb = np.clip(np.arange(ncb) * NA_QCOLS - NA_WIN_W // 2, 0, GRID_W - NA_KBAND)
    kcols = kb[:, None] + np.arange(NA_KBAND)[None, :]
    qc = qcol.reshape(ncb, NA_QCOLS)[:, :, None]
    cs = cstart.reshape(ncb, NA_QCOLS)[:, :, None]
    kc = kcols[:, None, :]
    valid = (kc >= cs) & (kc < cs + NA_WIN_W)
    dc_idx = np.clip(kc - qc + NA_WIN_W - 1, 0, 2 * NA_WIN_W - 2)
    return kcols, valid, dc_idx


def _neighborhood_attention(h, w_qkv, q_gain, k_gain, rpb, w_o):
    B_, S_, D_ = h.shape
    rows = S_ // GRID_W
    kh = min(NA_WIN_H, rows)
    ncb = GRID_W // NA_QCOLS
    kcols, valid, dc_idx = _na_column_tables()
    qkv = (h @ w_qkv).reshape(B_, rows, GRID_W, 3, NA_HEADS, NA_HEAD_DIM)
    q = _rms_norm(qkv[..., 0, :, :], q_gain) * (NA_HEAD_DIM ** -0.5)
    k = _rms_norm(qkv[..., 1, :, :], k_gain)
    v = qkv[..., 2, :, :]
    q = q.reshape(B_, rows, ncb, NA_QCOLS, NA_HEADS, NA_HEAD_DIM)
    rpb_col = rpb[:, :, dc_idx].astype(jnp.float32)
    mask = jnp.asarray(valid)[:, :, None, :]

    def row_step(r):
        r0 = jnp.clip(r - kh // 2, 0, rows - kh)
        q_r = lax.dynamic_index_in_dim(q, r, axis=1, keepdims=False)
        k_r = lax.dynamic_slice_in_dim(k, r0, kh, axis=1)[:, :, kcols]
        v_r = lax.dynamic_slice_in_dim(v, r0, kh, axis=1)[:, :, kcols]
        s = jnp.einsum("bnihd,bknjhd->bhnikj", q_r, k_r,
                       preferred_element_type=jnp.float32)
        dr_idx = r0 + jnp.arange(kh) - r + NA_WIN_H - 1
        bias = jnp.take(rpb_col, dr_idx, axis=1)
        s = s + jnp.transpose(bias, (0, 2, 3, 1, 4))
        s = jnp.where(mask, s, -jnp.inf)
        p = jax.nn.softmax(s.reshape(s.shape[:4] + (kh * NA_KBAND,)), axis=-1)
        p = p.reshape(s.shape).astype(v_r.dtype)
        o = jnp.einsum("bhnikj,bknjhd->bnihd", p, v_r)
        return o.reshape(B_, GRID_W, NA_HEADS, NA_HEAD_DIM)

    o = lax.map(row_step, jnp.arange(rows))
    o = jnp.transpose(o, (1, 0, 2, 3, 4)).reshape(B_, S_, D_)
    return o @ w_o


def _complex_affine_combine(e1, e2):
    a1r, a1i, b1r, b1i = e1
    a2r, a2i, b2r, b2i = e2
    ar = a2r * a1r - a2i * a1i
    ai = a2r * a1i + a2i * a1r
    br = a2r * b1r - a2i * b1i + b2r
    bi = a2r * b1i + a2i * b1r + b2i
    return (ar, ai, br, bi)


def _s5_direction(u, a_re, a_im, log_dt, b_re, b_im, c_re, c_im, reverse):
    f32 = jnp.float32
    lam_re = jnp.minimum(a_re.astype(f32), -1e-4)
    lam_im = a_im.astype(f32)
    dt = jnp.exp(log_dt.astype(f32))[:, None]
    mag = jnp.exp(lam_re * dt)
    ab_r = mag * jnp.cos(lam_im * dt)
    ab_i = mag * jnp.sin(lam_im * dt)
    den = jnp.square(lam_re) + jnp.square(lam_im)
    w_r = ab_r - 1.0
    w_i = ab_i
    f_r = (w_r * lam_re + w_i * lam_im) / den
    f_i = (w_i * lam_re - w_r * lam_im) / den
    br, bi = b_re.astype(f32), b_im.astype(f32)
    bb_r = f_r[..., None] * br - f_i[..., None] * bi
    bb_i = f_r[..., None] * bi + f_i[..., None] * br
    bu_r = jnp.einsum("bsgc,gpc->bsgp", u, bb_r)
    bu_i = jnp.einsum("bsgc,gpc->bsgp", u, bb_i)
    S_ = u.shape[1]
    a_r = jnp.broadcast_to(ab_r, (1, S_) + ab_r.shape)
    a_i = jnp.broadcast_to(ab_i, (1, S_) + ab_i.shape)
    _, _, x_r, x_i = lax.associative_scan(
        _complex_affine_combine, (a_r, a_i, bu_r, bu_i), reverse=reverse, axis=1)
    return (jnp.einsum("bsgp,gcp->bsgc", x_r, c_re.astype(f32))
            - jnp.einsum("bsgp,gcp->bsgc", x_i, c_im.astype(f32)))


def _s5_mixer(h, w_in, a_re, a_im, log_dt, b_re, b_im, c_re, c_im, d_skip, w_glu):
    B_, S_, D_ = h.shape
    u = (h @ w_in).astype(jnp.float32).reshape(B_, S_, S5_GROUPS, S5_GC)
    y = d_skip.astype(jnp.float32) * u
    for direction in range(2):
        y = y + _s5_direction(u, a_re[direction], a_im[direction], log_dt[direction],
                              b_re[direction], b_im[direction], c_re[direction],
                              c_im[direction], reverse=(direction == 1))
    y = jax.nn.gelu(y.reshape(B_, S_, D_)).astype(h.dtype)
    a, g = jnp.split(y @ w_glu, 2, axis=-1)
    return a * jax.nn.sigmoid(g)


def setup_inputs(seed: int = 0) -> dict:
    key = jax.random.key(seed)
    ks = iter(jax.random.split(key, 48))
    f32 = jnp.float32
    D = D_MODEL

    def nrm(shape, scale):
        return jax.random.normal(next(ks), shape, f32) * scale

    def gain(shape):
        return 1.0 + nrm(shape, 0.02)

    n0, n1, n2, n3 = [len(range(k, DEPTH, N_MIXERS)) for k in range(N_MIXERS)]
    G, P, Cg = S5_GROUPS, S5_STATE, S5_GC
    x = jax.random.normal(next(ks), (BATCH, SEQ, D), f32)
    mix_norm = gain((DEPTH, D))
    fnet_w = nrm((n0, D, D), D ** -0.5)
    fnet_b = nrm((n0, D), 0.01)
    conv_w_in = nrm((n1, D, 2 * D), D ** -0.5)
    conv_b_in = nrm((n1, 2 * D), 0.01)
    conv_w_dw = nrm((n1, CONV_WIDTH, D), CONV_WIDTH ** -0.5)
    conv_b_dw = nrm((n1, D), 0.01)
    conv_ln_g = gain((n1, D))
    conv_ln_b = nrm((n1, D), 0.01)
    conv_w_out = nrm((n1, D, D), D ** -0.5)
    conv_b_out = nrm((n1, D), 0.01)
    na_w_qkv = nrm((n2, D, 3 * D), D ** -0.5)
    na_q_gain = gain((n2, NA_HEAD_DIM))
    na_k_gain = gain((n2, NA_HEAD_DIM))
    na_rpb = nrm((n2, NA_HEADS, 2 * NA_WIN_H - 1, 2 * NA_WIN_W - 1), 0.1)
    na_w_o = nrm((n2, D, D), D ** -0.5)
    s5_w_in = nrm((n3, D, D), D ** -0.5)
    s5_a_re = -0.5 + nrm((n3, 2, G, P), 0.01)
    s5_a_im = jnp.pi * jnp.arange(P, dtype=f32) + nrm((n3, 2, G, P), 0.01)
    s5_log_dt = jax.random.uniform(next(ks), (n3, 2, G), f32,
                                   math.log(1e-3), math.log(1e-1))
    s5_b_re = nrm((n3, 2, G, P, Cg), (2 * Cg) ** -0.5)
    s5_b_im = nrm((n3, 2, G, P, Cg), (2 * Cg) ** -0.5)
    s5_c_re = nrm((n3, 2, G, Cg, P), P ** -0.5)
    s5_c_im = nrm((n3, 2, G, Cg, P), P ** -0.5)
    s5_d = nrm((n3, G, Cg), 1.0)
    s5_w_glu = nrm((n3, D, 2 * D), D ** -0.5)
    mlp_norm = gain((DEPTH, D))
    mlp_w1 = nrm((DEPTH, D, D_FF), D ** -0.5)
    mlp_w2 = nrm((DEPTH, D_FF, D), D_FF ** -0.5)
    return {"x": x, "mix_norm": mix_norm, "fnet_w": fnet_w, "fnet_b": fnet_b,
            "conv_w_in": conv_w_in, "conv_b_in": conv_b_in, "conv_w_dw": conv_w_dw,
            "conv_b_dw": conv_b_dw, "conv_ln_g": conv_ln_g, "conv_ln_b": conv_ln_b,
            "conv_w_out": conv_w_out, "conv_b_out": conv_b_out,
            "na_w_qkv": na_w_qkv, "na_q_gain": na_q_gain, "na_k_gain": na_k_gain,
            "na_rpb": na_rpb, "na_w_o": na_w_o,
            "s5_w_in": s5_w_in, "s5_a_re": s5_a_re, "s5_a_im": s5_a_im,
            "s5_log_dt": s5_log_dt, "s5_b_re": s5_b_re, "s5_b_im": s5_b_im,
            "s5_c_re": s5_c_re, "s5_c_im": s5_c_im, "s5_d": s5_d, "s5_w_glu": s5_w_glu,
            "mlp_norm": mlp_norm, "mlp_w1": mlp_w1, "mlp_w2": mlp_w2}


def reference(x, mix_norm, fnet_w, fnet_b, conv_w_in, conv_b_in, conv_w_dw, conv_b_dw,
              conv_ln_g, conv_ln_b, conv_w_out, conv_b_out, na_w_qkv, na_q_gain,
              na_k_gain, na_rpb, na_w_o, s5_w_in, s5_a_re, s5_a_im, s5_log_dt,
              s5_b_re, s5_b_im, s5_c_re, s5_c_im, s5_d, s5_w_glu,
              mlp_norm, mlp_w1, mlp_w2):
    for i in range(DEPTH):
        kind = i % N_MIXERS
        j = i // N_MIXERS
        h = _rms_norm(x, mix_norm[i])
        if kind == 0:
            m = _fourier_mixer(h, fnet_w[j], fnet_b[j])
        elif kind == 1:
            m = _conv_module(h, conv_w_in[j], conv_b_in[j], conv_w_dw[j], conv_b_dw[j],
                             conv_ln_g[j], conv_ln_b[j], conv_w_out[j], conv_b_out[j])
        elif kind == 2:
            m = _neighborhood_attention(h, na_w_qkv[j], na_q_gain[j], na_k_gain[j],
                                        na_rpb[j], na_w_o[j])
        else:
            m = _s5_mixer(h, s5_w_in[j], s5_a_re[j], s5_a_im[j], s5_log_dt[j],
                          s5_b_re[j], s5_b_im[j], s5_c_re[j], s5_c_im[j], s5_d[j],
                          s5_w_glu[j])
        x = x + m
        h = _rms_norm(x, mlp_norm[i])
        x = x + jnp.square(jax.nn.relu(h @ mlp_w1[i])) @ mlp_w2[i]
    return x
```

```python
import numpy as np
import concourse.bass as bass
import concourse.mybir as mybir
from concourse.bass_utils import run_bass_kernel_spmd

F32 = mybir.dt.float32
BF16 = mybir.dt.bfloat16
I32 = mybir.dt.int32
AF = mybir.ActivationFunctionType
ALU = mybir.AluOpType

D = 2048
S = 2048
FF = 8192
NT = 512
KC = 16
NCORES = 8
MAGIC = 12582912.0
TWO_PI = 6.283185307179586


class Tok:
    __slots__ = ("sem", "val")

    def __init__(self, sem, val):
        self.sem = sem
        self.val = val


class DSem:
    def __init__(self, sem):
        self.sem = sem
        self.cnt = 0


class Eng:
    def __init__(self, name, sem):
        self.name = name
        self.sem = sem
        self.cnt = 0
        self.ops = []
        self.waited = {}


class Prog:
    def __init__(self, nc, stack):
        self.nc = nc
        self.stack = stack
        self.eng = {}
        for n in ("sp", "act", "pool", "dve", "pe"):
            self.eng[n] = Eng(n, stack.enter_context(nc.semaphore("sem_" + n)))
        self.nds = 0
        self.all_ds = []

    def dsem(self):
        self.nds += 1
        d = DSem(self.stack.enter_context(self.nc.semaphore("ds%d" % self.nds)))
        self.all_ds.append(d)
        return d

    def barrier(self):
        toks = [Tok(e.sem, e.cnt) for e in self.eng.values() if e.cnt > 0]
        toks += [Tok(d.sem, d.cnt) for d in self.all_ds if d.cnt > 0]
        for en in self.eng:
            self.wait(en, toks)

    def _waits(self, e, deps):
        w = []
        for t in deps:
            if t is None:
                continue
            k = id(t.sem)
            if e.waited.get(k, 0) < t.val:
                e.waited[k] = t.val
                w.append((t.sem, t.val))
        return w

    def op(self, en, fn, deps=(), serial=True, signal=True):
        e = self.eng[en]
        deps = list(deps)
        if serial and en in ("act", "dve", "pool") and e.cnt > 0:
            deps.append(Tok(e.sem, e.cnt))
        w = self._waits(e, deps)
        if signal:
            e.cnt += 1
            tok = Tok(e.sem, e.cnt)
            inc = (e.sem, 1)
        else:
            tok = None
            inc = None
        e.ops.append((w, fn, inc))
        return tok

    def dma(self, qn, out, in_, ds, deps=(), slow=False):
        e = self.eng[qn]
        w = self._waits(e, deps)
        ds.cnt += 16
        if slow:
            e.ops.append((w, (lambda q, o=out, i=in_: q.dma_start(out=o, in_=i, allow_slow_non_contiguous=True)), (ds.sem, 16)))
        else:
            e.ops.append((w, (lambda q, o=out, i=in_: q.dma_start(out=o, in_=i)), (ds.sem, 16)))
        return Tok(ds.sem, ds.cnt)

    def wait(self, en, deps):
        e = self.eng[en]
        w = self._waits(e, deps)
        if w:
            e.ops.append((w, None, None))

    def emit(self):
        nc = self.nc
        with nc.Block() as block:
            def mk(e):
                def run(q):
                    for w, fn, inc in e.ops:
                        for sem, val in w:
                            q.wait_ge(sem, val)
                        if fn is not None:
                            ins = fn(q)
                            if inc is not None:
                                ins.then_inc(inc[0], inc[1])
                return run
            block.sync(mk(self.eng["sp"]))
            block.scalar(mk(self.eng["act"]))
            block.gpsimd(mk(self.eng["pool"]))
            block.vector(mk(self.eng["dve"]))
            block.tensor(mk(self.eng["pe"]))


ALL_INPUTS = [
    ("mix_norm", [4, D]), ("fnet_w", [D, D]), ("fnet_b", [D]),
    ("conv_w_in", [D, 2 * D]), ("conv_b_in", [2 * D]), ("conv_w_dw", [31, D]),
    ("conv_b_dw", [D]), ("conv_ln_g", [D]), ("conv_ln_b", [D]), ("conv_w_out", [D, D]),
    ("conv_b_out", [D]), ("na_w_qkv", [D, 3 * D]), ("na_q_gain", [128]), ("na_k_gain", [128]),
    ("na_rpb", [16, 15, 31]), ("na_w_o", [D, D]), ("s5_w_in", [D, D]),
    ("s5_a_re", [2, 128, 64]), ("s5_a_im", [2, 128, 64]), ("s5_log_dt", [2, 128]),
    ("s5_b_re", [2, 128, 64, 16]), ("s5_b_im", [2, 128, 64, 16]),
    ("s5_c_re", [2, 128, 16, 64]), ("s5_c_im", [2, 128, 16, 64]), ("s5_d", [128, 16]),
    ("s5_w_glu", [D, 2 * D]), ("mlp_norm", [4, D]),
    ("mlp_w1_0", [D, FF]), ("mlp_w2_0", [FF, D]), ("mlp_w1_1", [D, FF]), ("mlp_w2_1", [FF, D]),
    ("mlp_w1_2", [D, FF]), ("mlp_w2_2", [FF, D]), ("mlp_w1_3", [D, FF]), ("mlp_w2_3", [FF, D]),
]
PREFIX = {"fnet": "fnet_", "conv": "conv_", "na": "na_", "s5": "s5_"}


def needed_inputs(sublayers):
    names = ["mix_norm", "mlp_norm"]
    for sl in sublayers:
        if sl.startswith("mlp"):
            names += ["mlp_w1_" + sl[3:], "mlp_w2_" + sl[3:]]
        else:
            names += [n for n, _ in ALL_INPUTS if n.startswith(PREFIX[sl])]
    return [(n, sh) for n, sh in ALL_INPUTS if n in names]


class K:
    def __init__(self, nseq, sublayers, debug=False):
        import contextlib
        self.nseq = nseq
        self.sublayers = sublayers
        self.stack = contextlib.ExitStack()
        nc = self.nc = bass.Bass("TRN2", target_bir_lowering=False)
        st = self.stack
        self.P = Prog(nc, st)
        dt = lambda name, shape, kind, ty=F32: nc.dram_tensor(name, list(shape), ty, kind=kind).ap()
        I = "ExternalInput"
        self.x = dt("x", [nseq, S, D], I)
        self.out = dt("out", [nseq, S, D], "ExternalOutput")
        self.iw = {}
        self.in_names = []
        for name, shape in needed_inputs(sublayers):
            self.iw[name] = dt(name, shape, I)
            self.in_names.append(name)
        SK = "ExternalOutput" if debug else "Internal"
        self.XT = dt("XT", [nseq, D, S], SK)
        self.HT = dt("HTs", [nseq, D, S], SK, BF16)
        self.ZT = dt("ZTs", [nseq, D, S], SK, BF16)
        self.UT = dt("UTs", [nseq, D, S], SK, F32)
        self.CSd = dt("CSd", [S, S], SK, BF16)
        self.SSd = dt("SSd", [S, S], SK, BF16)
        self.RB = dt("RBs", [16, 15, 128], SK, F32)
        sb = lambda name, shape, ty: st.enter_context(nc.sbuf_tensor(name, list(shape), ty))
        self.arena = sb("arena", [128, 98304], BF16)
        self.xA = self.cv(0, [KC, NT], F32)
        self.hB = self.cv(32768, [KC, NT + 32], BF16)
        self.sqb = self.cv(50176, [KC, NT], BF16)
        self.C = self.cv(66560, [64, NT], BF16)
        self.wr = [self.cv(132096, [8192], BF16), self.cv(148480, [8192], BF16)]
        self.tmpf = [self.cv(164864 + 2048 * i, [NT], F32) for i in range(4)]
        self.rstd = self.cv(173056, [NT], F32)
        self.MISC = 175104
        self.ones = sb("ones", [128, 128], BF16)
        self.identf = sb("identf", [128, 128], F32)
        self.identb = sb("identb", [128, 128], BF16)
        self.iot = sb("iot", [128, 128], I32)
        self.gvec = sb("gvec", [128, 8, KC], F32)
        self.epsc = sb("epsc", [128, 1], F32)
        self.pidx = sb("pidx", [128, 1], F32)
        self.ps = st.enter_context(nc.psum_tensor("ps", [128, 8 * 512], F32))
        self.bank_free = [None] * 8
        self.bank_i = 0
        self.wr_ds = [self.P.dsem() for _ in range(2)]
        self.wr_free = [None, None]
        self.wr_i = 0
        self.ld_ds = self.P.dsem()
        self.st_ds = self.P.dsem()
        self.c_ds = self.P.dsem()
        self.last_store = {}

    def cv(self, off, shape, ty):
        n = 1
        for d_ in shape:
            n *= d_
        nb = n * (4 if ty in (F32, I32) else 2)
        a = self.arena[:, off // 2:(off + nb) // 2]
        if ty != BF16:
            a = a.bitcast(ty)
        if len(shape) == 2:
            a = a.rearrange("p (a b) -> p a b", a=shape[0])
        elif len(shape) == 3:
            a = a.rearrange("p (a b c) -> p a b c", a=shape[0], b=shape[1])
        return a

    def bank(self):
        b = self.bank_i
        self.bank_i = (b + 1) % 8
        return b, self.ps[:, b * 512:(b + 1) * 512]

    def wslot(self):
        i = self.wr_i
        self.wr_i = 1 - i
        return i

    def tt(self, en, out, a, b, op, deps=(), serial=True):
        return self.P.op(en, lambda e: e.tensor_tensor(out=out, in0=a, in1=b, op=op), deps, serial)

    def ts(self, en, out, a, s1, op0, s2=None, op1=None, deps=(), serial=True):
        if op1 is None:
            return self.P.op(en, lambda e: e.tensor_scalar(out=out, in0=a, scalar1=s1, scalar2=None, op0=op0), deps, serial)
        return self.P.op(en, lambda e: e.tensor_scalar(out=out, in0=a, scalar1=s1, scalar2=s2, op0=op0, op1=op1), deps, serial)

    def stt(self, en, out, a, sc, b, op0, op1, deps=(), serial=True):
        return self.P.op(en, lambda e: e.scalar_tensor_tensor(out=out, in0=a, scalar=sc, in1=b, op0=op0, op1=op1), deps, serial)

    def act(self, out, in_, func, bias=None, scale=None, deps=(), serial=True):
        kw = {}
        if bias is not None:
            kw["bias"] = bias
        if scale is not None:
            kw["scale"] = scale
        return self.P.op("act", lambda a: a.activation(out=out, in_=in_, func=func, **kw), deps, serial)

    def cp(self, en, out, in_, deps=(), serial=True):
        if en == "act":
            return self.P.op(en, lambda e: e.activation(out=out, in_=in_, func=AF.Identity), deps, serial)
        return self.P.op(en, lambda e: e.tensor_copy(out=out, in_=in_), deps, serial)

    def mm(self, out, lhsT, rhs, start, stop, deps=(), signal=False):
        return self.P.op("pe", lambda t: t.matmul(out, lhsT=lhsT, rhs=rhs, start=start, stop=stop), deps, signal=signal)

    def setup_consts(self):
        P = self.P
        P.op("pool", lambda g: g.memset(self.ones[:], 1.0))
        P.op("pool", lambda g: g.memset(self.epsc[:], 1e-6))
        P.op("pool", lambda g: g.iota(self.iot[:], pattern=[[1, 128]], base=0, channel_multiplier=-1))
        P.op("pool", lambda g: g.tensor_copy(out=self.identf[:], in_=self.iot[:]))
        P.op("pool", lambda g: g.tensor_single_scalar(out=self.identf[:], in_=self.identf[:], scalar=0.0, op=ALU.is_equal))
        P.op("pool", lambda g: g.tensor_copy(out=self.identb[:], in_=self.identf[:]))
        P.op("pool", lambda g: g.iota(self.iot[:, 0:1], pattern=[[0, 1]], base=0, channel_multiplier=1))
        self.t_const = P.op("pool", lambda g: g.tensor_copy(out=self.pidx[:], in_=self.iot[:, 0:1]))
        toks = []
        for j, nm in enumerate(("mix_norm", "mlp_norm")):
            for i in range(4):
                src = self.iw[nm][i, :].rearrange("(kc p) -> p kc", p=128)
                toks.append(P.dma("sp", self.gvec[:, j * 4 + i, :], src, self.c_ds, slow=True))
        self.t_gvec = toks[-1]

    def load_vec(self, dst, src1d):
        return self.P.dma("sp", dst, src1d.rearrange("(kc p) -> p kc", p=128), self.c_ds, slow=True)

    def iota_free(self, dst_f32, n, tmp_i32):
        P = self.P
        for c0 in range(0, n, 512):
            c1 = min(n, c0 + 512)
            P.op("pool", lambda g, c0=c0, c1=c1: g.iota(tmp_i32[:, c0:c1], pattern=[[1, c1 - c0]], base=c0, channel_multiplier=0))
        return P.op("pool", lambda g: g.tensor_copy(out=dst_f32, in_=tmp_i32))

    def trig(self, out, idx, scal, inv, shift, tmp1, tmp2, scale_out=None, deps=()):
        self.ts("dve", tmp1, idx, scal, ALU.mult, inv, ALU.mult, deps=list(deps) + [getattr(self, "trig_free", None)])
        if shift != 0.0:
            self.ts("dve", tmp1, tmp1, shift, ALU.add)
        self.ts("dve", tmp2, tmp1, MAGIC, ALU.add)
        self.ts("dve", tmp2, tmp2, MAGIC, ALU.subtract)
        t = self.tt("dve", tmp1, tmp1, tmp2, ALU.subtract)
        if scale_out is None:
            self.trig_free = self.act(out, tmp1, AF.Sin, scale=TWO_PI, deps=[t])
            return self.trig_free
        t = self.act(tmp2, tmp1, AF.Sin, scale=TWO_PI, deps=[t])
        self.trig_free = t
        return self.ts("dve", out, tmp2, scale_out, ALU.mult, deps=[t])

    def trig2(self, cos_out, sin_out, idx, scal, tmp1, tmp2):
        self.ts("dve", tmp1, idx, scal, ALU.mult, deps=[getattr(self, "trig_free", None)])
        self.ts("dve", tmp2, tmp1, MAGIC, ALU.add)
        self.ts("dve", tmp2, tmp2, MAGIC, ALU.subtract)
        t = self.tt("dve", tmp1, tmp1, tmp2, ALU.subtract)
        self.act(sin_out, tmp1, AF.Sin, scale=TWO_PI, deps=[t])
        self.ts("dve", tmp2, tmp1, 0.25, ALU.is_gt)
        t = self.stt("dve", tmp2, tmp1, 0.25, tmp2, ALU.add, ALU.subtract)
        self.trig_free = self.act(cos_out, tmp2, AF.Sin, scale=TWO_PI, deps=[t])
        return self.trig_free

    def transpose_in(self):
        P = self.P
        t_ev = None
        prev_st = None
        for s in range(self.nseq):
            for tt in range(S // 128):
                buf = self.xA[:, 0:4, :].rearrange("p a b -> p (a b)")
                t_ld = P.dma("sp", buf, self.x[s, tt * 128:(tt + 1) * 128, :], self.ld_ds, deps=[t_ev])
                obuf = self.xA[:, 4:8, :].rearrange("p a b -> p (a b)").rearrange("p (k t) -> p k t", k=KC)
                for q4 in range(4):
                    b, pb = self.bank()
                    for j in range(4):
                        k = q4 * 4 + j
                        tk = P.op("pe", lambda t, o=pb[:, j * 128:(j + 1) * 128], i=buf[:, k * 128:(k + 1) * 128]: t.transpose(o, i, self.identf[:]),
                                  deps=[t_ld, self.t_const, self.bank_free[b]], signal=(j == 3))
                    t_ev = self.cp("dve", obuf[:, q4 * 4:(q4 + 1) * 4, :], pb.rearrange("p (k t) -> p k t", k=4), deps=[tk, prev_st], serial=False)
                    self.bank_free[b] = t_ev
                dst = self.XT[s].rearrange("(kc p) t -> p kc t", p=128)[:, :, tt * 128:(tt + 1) * 128]
                prev_st = P.dma("sp", dst, obuf, self.st_ds, deps=[t_ev])
        self.xt_ready = prev_st

    def transpose_out(self):
        P = self.P
        t_ev = None
        last = None
        for s in range(self.nseq):
            for tt in range(S // 128):
                buf = self.xA[:, 0:4, :].rearrange("p a b -> p (a b)").rearrange("p (k t) -> p k t", k=KC)
                src = self.XT[s].rearrange("(kc p) t -> p kc t", p=128)[:, :, tt * 128:(tt + 1) * 128]
                t_ld = P.dma("sp", buf, src, self.ld_ds, deps=[t_ev, self.xt_ready])
                obuf = self.xA[:, 4:8, :].rearrange("p a b -> p (a b)")
                for q4 in range(4):
                    b, pb = self.bank()
                    for j in range(4):
                        k = q4 * 4 + j
                        tk = P.op("pe", lambda t, o=pb[:, j * 128:(j + 1) * 128], i=buf[:, k, :]: t.transpose(o, i, self.identf[:]),
                                  deps=[t_ld, self.t_const, self.bank_free[b]], signal=(j == 3))
                    t_ev = self.cp("dve", obuf[:, q4 * 512:(q4 + 1) * 512], pb, deps=[tk, last], serial=False)
                    self.bank_free[b] = t_ev
                last = P.dma("sp", self.out[s, tt * 128:(tt + 1) * 128, :], obuf, self.st_ds, deps=[t_ev])
        self.final_tok = last

    def xview(self, T, s, t0, n=NT):
        return T[s].rearrange("(kc p) t -> p kc t", p=128)[:, :, t0:t0 + n]

    def load_x_block(self, s, t0, deps=()):
        return self.P.dma("sp", self.xA[:], self.xview(self.XT, s, t0), self.ld_ds, deps=list(deps) + [self.xt_ready])

    def store_x_block(self, s, t0, deps=()):
        t = self.P.dma("sp", self.xview(self.XT, s, t0), self.xA[:], self.st_ds, deps=list(deps))
        self.xt_ready = t
        return t

    def rmsnorm_block(self, t_x, gidx, out3, deps_out=()):
        P = self.P
        t_sq = self.act(self.sqb[:], self.xA[:], AF.Square, deps=[t_x, self.sq_free])
        b, pb = self.bank()
        for k in range(KC):
            tk = self.mm(pb, self.ones[:], self.sqb[:, k, :], k == 0, k == KC - 1, deps=[t_sq, self.t_const, self.bank_free[b]], signal=(k == KC - 1))
        self.sq_free = tk
        t1 = self.act(self.rstd[:], pb, AF.Sqrt, bias=self.epsc[:, 0:1], scale=1.0 / D, deps=[tk, self.rstd_free])
        self.bank_free[b] = t1
        t2 = P.op("dve", lambda v: v.reciprocal(out=self.rstd[:], in_=self.rstd[:]), deps=[t1])
        t3 = None
        for k in range(KC):
            t3 = self.stt("dve", out3[:, k, :], self.xA[:, k, :], self.gvec[:, gidx, k:k + 1], self.rstd[:], ALU.mult, ALU.mult,
                          deps=[t2, self.t_gvec] + list(deps_out), serial=(k == 0))
        self.rstd_free = t3
        return t3

    def load_w(self, src3, shape3, deps=(), q="pool"):
        i = self.wslot()
        a, bsz = shape3
        view = self.wr[i][:, 0:a * bsz].rearrange("p (a b) -> p a b", a=a)
        t = self.P.dma(q, view, src3, self.wr_ds[i], deps=[self.wr_free[i]] + list(deps))
        return view, t, i

    def linear_block(self, w_dram, ncols, col0, src3, t_src, epilogue, kc=KC):
        wv3 = w_dram.rearrange("(kc p) f -> p kc f", p=128)
        nm = ncols // 128
        g = 4 if kc == 16 else 1
        tk = None
        for mg in range(0, nm, g):
            gg = min(g, nm - mg)
            wv, t_w, wi = self.load_w(wv3[:, :, col0 + mg * 128: col0 + (mg + gg) * 128], (kc, gg * 128))
            for mt in range(gg):
                b, pb = self.bank()
                for k in range(kc):
                    tk = self.mm(pb, wv[:, k, mt * 128:(mt + 1) * 128], src3[:, k, :], k == 0, k == kc - 1,
                                 deps=[t_w, t_src, self.bank_free[b]], signal=(k == kc - 1))
                self.bank_free[b] = epilogue(mg + mt, pb, tk)
            self.wr_free[wi] = tk
        return tk

    def mlp(self, li):
        w1 = self.iw["mlp_w1_%d" % li]
        w2 = self.iw["mlp_w2_%d" % li]
        for s in range(self.nseq):
            for tb in range(S // NT):
                t0 = tb * NT
                t_x = self.load_x_block(s, t0, deps=[self.xa_free])
                t_h = self.rmsnorm_block(t_x, 4 + li, self.hB[:, :, 0:NT], deps_out=[self.hb_free])
                st = {}

                def ep1(m, pb, tk):
                    j = m % 4
                    tr = self.act(self.tmpf[j][:], pb, AF.Relu, deps=[tk, self.tmp_free[j]])
                    st["c"] = self.tt("dve", self.C[:, m, :], self.tmpf[j][:], self.tmpf[j][:], ALU.mult, deps=[tr, self.c_free], serial=False)
                    self.tmp_free[j] = st["c"]
                    return tr
                tk = self.linear_block(w1, FF, 0, self.hB[:, :, 0:NT], t_h, ep1)
                self.hb_free = tk

                def ep2(m, pb, tk):
                    st["e"] = self.tt("dve", self.xA[:, m, :], pb, self.xA[:, m, :], ALU.add, deps=[tk, t_h], serial=False)
                    return st["e"]
                tk = self.linear_block(w2, D, 0, self.C, st["c"], ep2, kc=64)
                self.c_free = tk
                self.xa_free = self.store_x_block(s, t0, deps=[st["e"]])

    def fnet(self):
        P = self.P
        M = self.MISC
        ccsc = self.cv(M, [2, 512], BF16)
        bvec = self.cv(M + 2048, [KC], F32)
        t_b = self.load_vec(bvec, self.iw["fnet_b"])
        iof = self.cv(0, [2048], F32)
        tmp1 = self.cv(8192, [2048], F32)
        tmp2 = self.cv(16384, [2048], F32)
        tmpi = self.cv(24576, [2048], I32)
        svec = self.cv(M + 2112, [1], F32)
        t_io = self.iota_free(iof, 2048, tmpi)
        NORM = 1.0 / (16.0 * (2048.0 ** 0.5))
        for kc in range(2):
            self.ts("dve", svec, self.pidx[:], float(kc * 128), ALU.add, deps=[t_io, self.t_const])
            self.trig(ccsc[:, kc, 0:256], iof[:, 0:256], svec[:, 0:1], 1.0 / 256, 0.25, tmp1[:, 0:256], tmp2[:, 0:256], scale_out=NORM)
            t_cc = self.trig(ccsc[:, kc, 256:512], iof[:, 0:256], svec[:, 0:1], 1.0 / 256, 0.5, tmp1[:, 0:256], tmp2[:, 0:256], scale_out=NORM)
        ob = [self.cv(50176, [2048], BF16), self.cv(50176 + 4096, [2048], BF16)]
        t_st = [None, None]
        for tt in range(16):
            self.ts("dve", svec, self.pidx[:], float(tt * 128), ALU.add)
            for j, (dst, shift) in enumerate(((self.CSd, 0.25), (self.SSd, 0.0))):
                P.wait("dve", [t_st[j]])
                P.wait("act", [t_st[j]])
                t = self.trig(ob[j], iof, svec[:, 0:1], 1.0 / 2048, shift, tmp1, tmp2)
                t_st[j] = P.dma("sp", dst[tt * 128:(tt + 1) * 128, :], ob[j], self.st_ds, deps=[t])
        P.barrier()
        self.sq_free = self.rstd_free = None
        t_stp = None
        for s in range(self.nseq):
            for tb in range(S // NT):
                t_x = self.load_x_block(s, tb * NT, deps=[self.xa_free])
                t_h = self.rmsnorm_block(t_x, 0, self.hB[:, :, 0:NT], deps_out=[t_stp])
                self.xa_free = t_h
                t_stp = P.dma("sp", self.xview(self.HT, s, tb * NT), self.hB[:, :, 0:NT], self.st_ds, deps=[t_h])
        P.barrier()
        hg = self.C[:, 0:8, :].rearrange("p a b -> p (a b)").rearrange("p (k t) -> p k t", k=2)
        pq = self.hB[:, :, 0:512]
        zst = [self.cv(0, [2, 512], BF16), self.cv(2048, [2, 512], BF16)]
        zst_free = [None, None]
        hg_free = None
        pq_free = None
        zi = 0
        csv = self.CSd.rearrange("(tt p) s -> p tt s", p=128)
        ssv = self.SSd.rearrange("(tt p) s -> p tt s", p=128)
        for s in range(self.nseq):
            for g in range(8):
                t_hg = P.dma("sp", hg, self.HT[s, g * 256:(g + 1) * 256, :].rearrange("(k p) t -> p k t", p=128), self.ld_ds, deps=[hg_free])
                t_pq = None
                for tt in range(16):
                    b, pb = self.bank()
                    for kc in range(2):
                        tk = self.mm(pb, hg[:, kc, tt * 128:(tt + 1) * 128], ccsc[:, kc, :], kc == 0, kc == 1,
                                     deps=[t_hg, t_cc, self.bank_free[b]], signal=(kc == 1))
                    t_pq = self.cp("act" if tt % 2 else "dve", pq[:, tt, :], pb, deps=[tk, pq_free], serial=False)
                    self.bank_free[b] = t_pq
                t_pq2 = Tok(P.eng["dve"].sem, P.eng["dve"].cnt)
                t_pq3 = Tok(P.eng["act"].sem, P.eng["act"].cnt)
                hg_free = tk
                for sbk in range(4):
                    cv_, t_c, ci = self.load_w(csv[:, :, sbk * 512:(sbk + 1) * 512], (16, 512), q="sp")
                    sv_, t_s, si = self.load_w(ssv[:, :, sbk * 512:(sbk + 1) * 512], (16, 512), q="act")
                    zb = zst[zi]
                    for ct in range(2):
                        b, pb = self.bank()
                        for tt in range(16):
                            self.mm(pb, pq[:, tt, ct * 128:(ct + 1) * 128], cv_[:, tt, :], tt == 0, False,
                                    deps=[t_c, t_pq2, t_pq3, self.bank_free[b]])
                        for tt in range(16):
                            tk = self.mm(pb, pq[:, tt, 256 + ct * 128:256 + (ct + 1) * 128], sv_[:, tt, :], False, tt == 15,
                                         deps=[t_s], signal=(tt == 15))
                        t_z = self.cp("act", zb[:, ct, :], pb, deps=[tk, zst_free[zi]])
                        self.bank_free[b] = t_z
                    self.wr_free[ci] = tk
                    self.wr_free[si] = tk
                    zst_free[zi] = P.dma("sp", self.ZT[s, g * 256:(g + 1) * 256, sbk * 512:(sbk + 1) * 512].rearrange("(k p) t -> p k t", p=128),
                                         zb, self.st_ds, deps=[t_z])
                    zi = 1 - zi
                pq_free = tk
        P.barrier()
        hb_free = None
        for s in range(self.nseq):
            for tb in range(S // NT):
                t_x = self.load_x_block(s, tb * NT, deps=[self.xa_free])
                t_z = P.dma("sp", self.hB[:, :, 0:NT], self.xview(self.ZT, s, tb * NT), self.ld_ds, deps=[hb_free])
                st = {}

                def ep(m, pb, tk):
                    st["e"] = self.stt("dve", self.xA[:, m, :], pb, bvec[:, m:m + 1], self.xA[:, m, :], ALU.add, ALU.add, deps=[tk, t_x, t_b], serial=False)
                    return st["e"]
                hb_free = self.linear_block(self.iw["fnet_w"], D, 0, self.hB[:, :, 0:NT], t_z, ep)
                self.xa_free = self.store_x_block(s, tb * NT, deps=[st["e"]])
        P.barrier()
        self.sq_free = self.rstd_free = self.hb_free = self.c_free = None

    def conv(self):
        P = self.P
        M = self.MISC
        bin_ = self.cv(M, [32], F32)
        bdw = self.cv(M + 128, [KC], F32)
        lng = self.cv(M + 192, [KC], F32)
        lnb = self.cv(M + 256, [KC], F32)
        bout = self.cv(M + 320, [KC], F32)
        wdw = self.cv(M + 384, [KC, 31], F32)
        dg = self.cv(M + 384 + 1984, [31, 128], BF16)
        mean = self.cv(M + 10304, [NT], F32)
        nmr = self.cv(M + 12352, [NT], F32)
        rs2 = self.cv(M + 14400, [NT], F32)
        self.load_vec(bin_, self.iw["conv_b_in"])
        self.load_vec(bdw, self.iw["conv_b_dw"])
        self.load_vec(lng, self.iw["conv_ln_g"])
        self.load_vec(lnb, self.iw["conv_ln_b"])
        self.load_vec(bout, self.iw["conv_b_out"])
        for m_ in range(16):
            t_p = P.dma("sp", wdw[:, m_, :], self.iw["conv_w_dw"][:, m_ * 128:(m_ + 1) * 128].rearrange("k p -> p k"), self.c_ds, slow=True)
        U = self.C[:, 0:16, :]
        u_free = None
        w_in = self.iw["conv_w_in"]
        for s in range(self.nseq):
            for tb in range(S // NT):
                t_x = self.load_x_block(s, tb * NT, deps=[self.xa_free])
                t_h = self.rmsnorm_block(t_x, 1, self.hB[:, :, 0:NT], deps_out=[self.hb_free])
                self.xa_free = t_h
                st = {}

                def ep_g(m, pb, tk):
                    j = m % 4
                    st[m] = self.act(self.tmpf[j][:], pb, AF.Sigmoid, bias=bin_[:, 16 + m:17 + m], deps=[tk, self.tmp_free[j], t_p])
                    return st[m]

                def ep_a(m, pb, tk):
                    j = m % 4
                    st["u"] = self.stt("dve", U[:, m, :], pb, bin_[:, m:m + 1], self.tmpf[j][:], ALU.add, ALU.mult, deps=[tk, st[m], u_free], serial=False)
                    self.tmp_free[j] = st["u"]
                    return st["u"]
                for mg in range(4):
                    self.linear_block(w_in, 512, D + mg * 512, self.hB[:, :, 0:NT], t_h, lambda m, pb, tk, mg=mg: ep_g(mg * 4 + m, pb, tk))
                    self.hb_free = self.linear_block(w_in, 512, mg * 512, self.hB[:, :, 0:NT], t_h, lambda m, pb, tk, mg=mg: ep_a(mg * 4 + m, pb, tk))
                u_free = P.dma("sp", self.xview(self.HT, s, tb * NT), U, self.st_ds, deps=[st["u"]])
        P.barrier()
        self.sq_free = self.rstd_free = self.hb_free = None
        V = self.C[:, 0:32, :].bitcast(F32).rearrange("p a b -> p (a b)").rearrange("p (m t) -> p m t", m=16)
        hb_free = None
        v_free = None
        dg_free = None
        w_out = self.iw["conv_w_out"]
        for s in range(self.nseq):
            for tb in range(S // NT):
                t0 = tb * NT
                lo = max(t0 - 15, 0)
                hi = min(t0 + NT + 15, S)
                off = lo - (t0 - 15)
                deps = [hb_free]
                if t0 == 0:
                    deps.append(P.op("pool", lambda g: g.memset(self.hB[:, :, 0:15], 0.0), deps=[hb_free]))
                if t0 + NT == S:
                    deps.append(P.op("pool", lambda g: g.memset(self.hB[:, :, NT + 15:NT + 30], 0.0), deps=[hb_free]))
                t_u = P.dma("sp", self.hB[:, :, off:off + hi - lo], self.xview(self.HT, s, lo, hi - lo), self.ld_ds, deps=deps)
                t_x = self.load_x_block(s, t0, deps=[self.xa_free])
                t_v = None
                for m in range(16):
                    for k in range(31):
                        t_d = self.ts("dve", dg[:, k, :], self.identb[:], wdw[:, m, k:k + 1], ALU.mult, deps=[dg_free, t_p, self.t_const], serial=False)
                    b, pb = self.bank()
                    for k in range(31):
                        tk = self.mm(pb, dg[:, k, :], self.hB[:, m, k:k + NT], k == 0, k == 30, deps=[t_d, t_u, self.bank_free[b]], signal=(k == 30))
                    dg_free = tk
                    t_v = self.act(V[:, m, :], pb, AF.Identity, bias=bdw[:, m:m + 1], deps=[tk, v_free])
                    self.bank_free[b] = t_v
                hb_free = tk
                t_sq = self.act(self.sqb[:], V, AF.Square, deps=[t_v, self.sq_free])
                t_vb = self.cp("dve", self.hB[:, :, 0:NT], V, deps=[t_v, tk])
                b1, pb1 = self.bank()
                b2, pb2 = self.bank()
                for m in range(16):
                    tk1 = self.mm(pb1, self.ones[:], self.hB[:, m, 0:NT], m == 0, m == 15, deps=[t_vb, self.bank_free[b1]], signal=(m == 15))
                for m in range(16):
                    tk2 = self.mm(pb2, self.ones[:], self.sqb[:, m, :], m == 0, m == 15, deps=[t_sq, self.bank_free[b2]], signal=(m == 15))
                self.sq_free = tk2
                t_m = self.act(mean, pb1, AF.Identity, scale=1.0 / D, deps=[tk1])
                self.bank_free[b1] = t_m
                self.tt("dve", nmr, mean, mean, ALU.mult, deps=[t_m])
                t_var = self.stt("dve", rs2, pb2, 1.0 / D, nmr, ALU.mult, ALU.subtract, deps=[tk2])
                self.bank_free[b2] = t_var
                t_sd = self.act(rs2, rs2, AF.Sqrt, bias=self.epsc[:, 0:1], deps=[t_var])
                P.op("dve", lambda v: v.reciprocal(out=rs2, in_=rs2), deps=[t_sd])
                self.tt("dve", nmr, mean, rs2, ALU.mult)
                t_n = self.ts("dve", nmr, nmr, -1.0, ALU.mult)
                t_s = None
                for m in range(16):
                    self.tt("dve", V[:, m, :], V[:, m, :], rs2, ALU.mult, deps=[t_n])
                    t_a = self.tt("dve", V[:, m, :], V[:, m, :], nmr, ALU.add)
                    t_s = self.act(self.hB[:, m, 0:NT], V[:, m, :], AF.Silu, bias=lnb[:, m:m + 1], scale=lng[:, m:m + 1], deps=[t_a, tk1])
                v_free = t_s
                st = {}

                def ep(m, pb, tk):
                    st["e"] = self.stt("dve", self.xA[:, m, :], pb, bout[:, m:m + 1], self.xA[:, m, :], ALU.add, ALU.add, deps=[tk, t_x], serial=False)
                    return st["e"]
                hb_free = self.linear_block(w_out, D, 0, self.hB[:, :, 0:NT], t_s, ep)
                self.xa_free = self.store_x_block(s, t0, deps=[st["e"]])
        P.barrier()
        self.sq_free = self.rstd_free = self.hb_free = self.c_free = None
        self.tmp_free = [None] * 4

    def na(self):
        P = self.P
        nc = self.nc
        M = self.MISC
        gq = self.cv(M, [1], F32)
        gk = self.cv(M + 4, [1], F32)
        mk = self.cv(M + 64, [64], F32)
        kv = self.cv(M + 320, [1], F32)
        t1 = self.cv(M + 384, [64], F32)
        t2 = self.cv(M + 640, [64], F32)
        ti = self.cv(M + 896, [64], I32)
        toep = self.cv(M + 1280, [15, 64], F32)
        rcp = self.cv(M + 5120, [64], F32)
        zer = self.cv(M + 5376, [1920], F32)
        ebuf = self.cv(M + 13056, [8, 64], BF16)
        sbuf_ = self.cv(M + 14080, [8, 64], F32)
        sqh = self.cv(M + 16128, [NT], BF16)
        rsq = self.cv(M + 17152, [NT], F32)
        toep_tmp = self.cv(M + 5376, [15, 64], F32)
        wq3 = self.cv(0, [16, 3, 128], BF16)
        qT = self.cv(12288, [S], BF16)
        kT = self.cv(16384, [S], BF16)
        vrow = self.cv(20480, [32, 128], BF16)
        oT = self.cv(28672, [S], BF16)
        hT = self.C[:].rearrange("p a b -> p (a b)").rearrange("p (k t) -> p k t", k=16)
        t_g = P.dma("sp", gq, self.iw["na_q_gain"].rearrange("(p o) -> p o", o=1), self.c_ds, slow=True)
        t_g = P.dma("sp", gk, self.iw["na_k_gain"].rearrange("(p o) -> p o", o=1), self.c_ds, slow=True)
        t_gq = self.ts("dve", gq, gq, 128.0 ** -0.5, ALU.mult, deps=[t_g])
        P.op("pool", lambda g: g.memset(zer, 0.0))
        tz = Tok(P.eng["pool"].sem, P.eng["pool"].cnt)
        t_rb = P.dma("sp", self.RB.rearrange("h r c -> h (r c)"), zer[0:16, :], self.st_ds, deps=[tz])
        rpb = self.iw["na_rpb"]
        t_rb = P.dma("sp", self.RB[:, :, 48:79], rpb, self.st_ds, deps=[t_rb], slow=True)
        self.iota_free(t1, 64, ti)
        t_k = self.cp("pool", kv, self.pidx[:], deps=[self.t_const])
        t_k = P.op("pool", lambda g: g.tensor_scalar(out=kv[64:128, :], in0=kv[64:128, :], scalar1=-64.0, scalar2=None, op0=ALU.add))
        self.ts("dve", t1, t1, -8.0, ALU.add, 0.0, ALU.max, deps=[t_k])
        self.ts("dve", t1, t1, 48.0, ALU.min)
        self.ts("dve", t1, t1, kv[:, 0:1], ALU.subtract, -1.0, ALU.mult)
        self.ts("dve", t2, t1, 0.0, ALU.is_ge)
        self.ts("dve", t1, t1, 16.0, ALU.is_lt)
        self.tt("dve", t1, t1, t2, ALU.mult)
        t_mk = self.ts("dve", mk, t1, -1.0, ALU.add, 30000.0, ALU.mult)
        P.barrier()
        wq = self.iw["na_w_qkv"].rearrange("(kc p) (three h d) -> p kc three h d", p=128, three=3, h=16)
        for s in range(self.nseq):
            self.sq_free = self.rstd_free = None
            for tb in range(S // NT):
                t_x = self.load_x_block(s, tb * NT, deps=[self.xa_free])
                t_h = self.rmsnorm_block(t_x, 2, hT[:, :, tb * NT:(tb + 1) * NT])
                self.xa_free = t_h
            P.barrier()
            import os as _os
            for hd in range(int(_os.environ.get('NA_HEADS', '16'))):
                for wh in range(3):
                    t_w = P.dma("pool", wq3[:, :, wh, :], wq[:, :, wh, hd, :], self.ld_ds)
                for hf_ in range(2):
                    t_tp = P.dma("sp", toep_tmp[hf_ * 64:(hf_ + 1) * 64], bass.AP(tensor=self.RB.tensor, offset=self.RB.offset + hd * 15 * 128, ap=[[1, 64], [128, 15], [1, 64]]),
                                 self.c_ds, deps=[t_rb])
                t_tp = self.tt("dve", toep, toep_tmp[:, :, ::-1], mk.unsqueeze(1).to_broadcast([128, 15, 64]), ALU.add, deps=[t_tp, t_mk])
                for which, dst, gain in ((0, qT, gq), (1, kT, gk)):
                    for tb in range(4):
                        b, pb = self.bank()
                        for k in range(KC):
                            tk = self.mm(pb, wq3[:, k, which, :], hT[:, k, tb * NT:(tb + 1) * NT], k == 0, k == KC - 1, deps=[t_w, self.bank_free[b]], signal=(k == KC - 1))
                        t_s = self.act(sqh, pb, AF.Square, deps=[tk])
                        b2, pb2 = self.bank()
                        tk2 = self.mm(pb2, self.ones[:], sqh, True, True, deps=[t_s, self.bank_free[b2]], signal=True)
                        t_r = self.act(rsq, pb2, AF.Sqrt, bias=self.epsc[:, 0:1], scale=1.0 / 128, deps=[tk2, getattr(self, "naq_free", None)])
                        self.bank_free[b2] = t_r
                        P.op("dve", lambda v: v.reciprocal(out=rsq, in_=rsq), deps=[t_r])
                        t_q = self.stt("dve", dst[:, tb * NT:(tb + 1) * NT], pb, gain[:, 0:1], rsq, ALU.mult, ALU.mult, deps=[t_gq])
                        self.bank_free[b] = t_q
                        self.naq_free = t_q
                for R4 in range(8):
                    b, pb = self.bank()
                    for j in range(4):
                        R = R4 * 4 + j
                        for k in range(KC):
                            tk = self.mm(pb[0:64, j * 128:(j + 1) * 128], hT[:, k, R * 64:(R + 1) * 64], wq3[:, k, 2, :], k == 0, k == KC - 1,
                                         deps=[t_w, self.bank_free[b]], signal=(j == 3 and k == KC - 1))
                    t_v = self.cp("act", vrow[0:64, R4 * 4:(R4 + 1) * 4, :], pb[0:64, :].rearrange("p (a b) -> p a b", a=4), deps=[tk])
                    self.bank_free[b] = t_v
                t_qkv = Tok(P.eng["dve"].sem, P.eng["dve"].cnt)
                t_o = None
                import os as _os
                for r in range(int(_os.environ.get('NA_ROWS', '32'))):
                    r0 = min(max(r - 4, 0), 24)
                    v_ = r - r0
                    d0 = 7 - v_
                    b, pb = self.bank()
                    for i in range(8):
                        R = r0 + i
                        tk = self.mm(pb[0:64, i * 64:(i + 1) * 64], kT[:, R * 64:(R + 1) * 64], qT[:, r * 64:(r + 1) * 64], True, True,
                                     deps=[t_qkv, t_v, self.bank_free[b]], signal=(i == 7))
                    t_a = self.tt("dve", sbuf_[0:64], pb[0:64, :].rearrange("p (a b) -> p a b", a=8), toep[0:64, d0:d0 + 8, :], ALU.add, deps=[tk, t_tp])
                    self.bank_free[b] = t_a
                    t_e = self.act(ebuf[0:64], sbuf_[0:64], AF.Exp, deps=[t_a, t_o])
                    b2, pb2 = self.bank()
                    for i in range(8):
                        R = r0 + i
                        self.mm(pb2[:, 0:64], vrow[0:64, R, :], ebuf[0:64, i, :], i == 0, i == 7, deps=[t_e, self.bank_free[b2]])
                    for i in range(8):
                        tk = self.mm(pb2[:, 64:128], self.ones[0:64, :], ebuf[0:64, i, :], i == 0, i == 7, signal=(i == 7))
                    t_o = P.op("dve", lambda v, pb2=pb2: v.reciprocal(out=rcp, in_=pb2[:, 64:128]), deps=[tk])
                    t_o = self.tt("dve", oT[:, r * 64:(r + 1) * 64], pb2[:, 0:64], rcp, ALU.mult)
                    self.bank_free[b2] = t_o
                t_st = P.dma("sp", self.HT[s, hd * 128:(hd + 1) * 128, :], oT, self.st_ds, deps=[t_o])
                P.barrier()
            hb_free = None
            for tb in range(S // NT):
                t_x = self.load_x_block(s, tb * NT, deps=[self.xa_free])
                t_z = P.dma("sp", self.hB[:, :, 0:NT], self.xview(self.HT, s, tb * NT), self.ld_ds, deps=[hb_free])
                st = {}

                def ep(m, pb, tk):
                    st["e"] = self.tt("dve", self.xA[:, m, :], pb, self.xA[:, m, :], ALU.add, deps=[tk, t_x], serial=False)
                    return st["e"]
                hb_free = self.linear_block(self.iw["na_w_o"], D, 0, self.hB[:, :, 0:NT], t_z, ep)
                self.xa_free = self.store_x_block(s, tb * NT, deps=[st["e"]])
            P.barrier()
        self.sq_free = self.rstd_free = self.hb_free = self.c_free = None

    def s5(self):
        P = self.P
        M = self.MISC
        SM = 32768

        def small(i):
            return self.cv(SM + 512 * i, [128], F32)
        are, aim, ldt, lre, dtv, mag, thx, cs, sn, abr, abi, den, fr, fi, ta, tb_ = [small(i) for i in range(16)]
        rho = [self.cv(M + 512 * d, [128], F32) for d in range(2)]
        thn = [self.cv(M + 1024 + 512 * d, [128], F32) for d in range(2)]
        dvec = self.cv(M + 2048, [KC], F32)
        msk = self.cv(M + 2112, [8, 128], F32)
        iof = self.cv(M + 6208, [2048], F32)
        tmpi = self.cv(M + 14400, [1024], I32)
        Cb = 66560
        X1 = [self.cv(Cb + 16384 * d, [2048], F32) for d in range(2)]
        X2 = [self.cv(Cb + 16384 * d + 8192, [2048], F32) for d in range(2)]
        W1 = [self.cv(Cb + 32768 + 16384 * d, [2048], F32) for d in range(2)]
        W2 = [self.cv(Cb + 32768 + 16384 * d + 8192, [2048], F32) for d in range(2)]
        Ust = self.C[:, 0:32, :].bitcast(F32).rearrange("p a b -> p (a b)").rearrange("p (m t) -> p m t", m=16)
        u_free = None
        for s in range(self.nseq):
            for tb in range(S // NT):
                t_x = self.load_x_block(s, tb * NT, deps=[self.xa_free])
                t_h = self.rmsnorm_block(t_x, 3, self.hB[:, :, 0:NT], deps_out=[self.hb_free])
                self.xa_free = t_h
                st = {}

                def ep(m, pb, tk):
                    st["u"] = self.cp("act", Ust[:, m, :], pb, deps=[tk, u_free])
                    return st["u"]
                self.hb_free = self.linear_block(self.iw["s5_w_in"], D, 0, self.hB[:, :, 0:NT], t_h, ep)
                u_free = P.dma("sp", self.xview(self.UT, s, tb * NT), Ust, self.st_ds, deps=[st["u"]])
        P.barrier()
        Bre = self.cv(0, [128, 16], F32)
        Bim = self.cv(8192, [128, 16], F32)
        T1 = self.cv(16384, [128, 16], F32)
        T2 = self.cv(24576, [128, 16], F32)
        cn = self.cv(164864, [2, 64], F32)
        t_io = self.iota_free(iof, 2048, self.cv(0, [2048], I32))
        mi = self.cv(8192, [8, 128], I32)
        mt_ = self.cv(16384, [8, 128], F32)
        P.op("pool", lambda g: g.iota(mi, pattern=[[-16, 8], [1, 128]], base=0, channel_multiplier=0))
        t_mi = self.cp("pool", msk, mi)
        self.ts("dve", mt_, msk, 0.0, ALU.is_ge, deps=[t_mi])
        self.ts("dve", msk, msk, 16.0, ALU.is_lt)
        t_msk = self.tt("dve", msk, msk, mt_, ALU.mult)
        t_d = P.dma("sp", dvec, self.iw["s5_d"].rearrange("(t g) c -> (g c) t", g=8), self.c_ds, slow=True)
        P.barrier()
        for d in range(2):
            toks = []
            for hf in range(2):
                toks.append(P.dma("sp", are[hf * 64:(hf + 1) * 64, :], self.iw["s5_a_re"][d].rearrange("g p -> p g"), self.c_ds, slow=True))
                toks.append(P.dma("sp", aim[hf * 64:(hf + 1) * 64, :], self.iw["s5_a_im"][d].rearrange("g p -> p g"), self.c_ds, slow=True))
                toks.append(P.dma("sp", Bre[hf * 64:(hf + 1) * 64], self.iw["s5_b_re"][d].rearrange("g p c -> p g c"), self.c_ds))
                toks.append(P.dma("sp", Bim[hf * 64:(hf + 1) * 64], self.iw["s5_b_im"][d].rearrange("g p c -> p g c"), self.c_ds))
            ld = self.iw["s5_log_dt"]
            toks.append(P.dma("sp", ldt, bass.AP(tensor=ld.tensor, offset=ld.offset + d * 128, ap=[[0, 128], [1, 128]]), self.c_ds, slow=True))
            self.ts("dve", lre, are, -1e-4, ALU.min, deps=toks)
            t = self.act(dtv, ldt, AF.Exp, deps=toks)
            self.tt("dve", ta, lre, dtv, ALU.mult, deps=[t])
            t = self.act(mag, ta, AF.Exp, deps=[Tok(P.eng["dve"].sem, P.eng["dve"].cnt)])
            self.cp("dve", rho[d], mag, deps=[t])
            self.tt("dve", thx, aim, dtv, ALU.mult)
            self.ts("dve", thn[d], thx, 1.0 / TWO_PI, ALU.mult)
            t = self.trig(sn, thn[d], 1.0, 1.0, 0.0, ta, tb_)
            t = self.trig(cs, thn[d], 1.0, 1.0, 0.25, ta, tb_, deps=[t])
            self.tt("dve", abr, mag, cs, ALU.mult, deps=[t])
            self.tt("dve", abi, mag, sn, ALU.mult)
            self.tt("dve", ta, lre, lre, ALU.mult)
            self.tt("dve", tb_, aim, aim, ALU.mult)
            self.tt("dve", den, ta, tb_, ALU.add)
            P.op("dve", lambda v: v.reciprocal(out=den, in_=den))
            self.ts("dve", abr, abr, -1.0, ALU.add)
            self.tt("dve", ta, abr, lre, ALU.mult)
            self.tt("dve", tb_, abi, aim, ALU.mult)
            self.tt("dve", ta, ta, tb_, ALU.add)
            self.tt("dve", fr, ta, den, ALU.mult)
            self.tt("dve", ta, abi, lre, ALU.mult)
            self.tt("dve", tb_, abr, aim, ALU.mult)
            self.tt("dve", ta, ta, tb_, ALU.subtract)
            self.tt("dve", fi, ta, den, ALU.mult)
            frb = fr.unsqueeze(2).to_broadcast([128, 128, 16])
            fib = fi.unsqueeze(2).to_broadcast([128, 128, 16])
            x1v = X1[d].rearrange("p (g c) -> p g c", c=16)
            x2v = X2[d].rearrange("p (g c) -> p g c", c=16)
            self.tt("dve", T1, Bre, frb, ALU.mult)
            self.tt("dve", T2, Bim, fib, ALU.mult)
            self.tt("dve", T1, T1, T2, ALU.subtract)
            self.tt("dve", T2, Bim, frb, ALU.mult)
            self.tt("dve", Bim, Bre, fib, ALU.mult)
            self.tt("dve", T2, T2, Bim, ALU.add)
            self.cp("dve", x1v[0:64], T1[0:64])
            self.cp("dve", x1v[64:128], T2[64:128])
            self.cp("dve", x2v[0:64], T2[0:64])
            self.ts("dve", x2v[64:128], T1[64:128], -1.0, ALU.mult)
            for T in range(16):
                for ri, nm in enumerate(("s5_c_re", "s5_c_im")):
                    srcn = self.iw[nm][d, T * 8:(T + 1) * 8].rearrange("g c p -> (g c) p")
                    P.wait("sp", [Tok(P.eng["pe"].sem, P.eng["pe"].cnt)])
                    t0_ = P.dma("sp", cn[:, 0, :], srcn, self.ld_ds)
                    t1_ = P.dma("sp", cn[:, 1, :], srcn, self.ld_ds)
                    b, pb = self.bank()
                    tk = P.op("pe", lambda t, pb=pb: t.transpose(pb[:, 0:128], cn.rearrange("p a b -> p (a b)"), self.identf[:]),
                              deps=[t0_, t1_, self.bank_free[b], self.t_const], signal=True)
                    w1v = W1[d][:, T * 128:(T + 1) * 128]
                    w2v = W2[d][:, T * 128:(T + 1) * 128]
                    if ri == 0:
                        self.cp("dve", w1v[0:64], pb[0:64, 0:128], deps=[tk])
                        te = self.ts("dve", w2v[64:128], pb[64:128, 0:128], -1.0, ALU.mult)
                    else:
                        self.ts("dve", w1v[64:128], pb[64:128, 0:128], -1.0, ALU.mult, deps=[tk])
                        te = self.ts("dve", w2v[0:64], pb[0:64, 0:128], -1.0, ALU.mult)
                    self.bank_free[b] = te
            P.barrier()
        cosT = self.cv(0, [2048], F32)
        sinT = self.cv(8192, [2048], F32)
        St = self.cv(16384, [2048], F32)
        Xt = self.cv(24576, [2048], F32)
        Pc = self.cv(32768, [2048], BF16)
        Ps = self.cv(36864, [2048], BF16)
        ub = self.cv(40960, [2048], BF16)
        uf = self.cv(45056, [2048], F32)
        yt = self.cv(53248, [2048], F32)
        yg = self.cv(61440, [2048], BF16)
        Wb = 132096
        LB1 = [self.cv(Wb + 4096 * d, [8, 128], BF16) for d in range(2)]
        LB2 = [self.cv(Wb + 4096 * d + 2048, [8, 128], BF16) for d in range(2)]
        WP1 = [self.cv(Wb + 8192 + 4096 * d, [8, 128], BF16) for d in range(2)]
        WP2 = [self.cv(Wb + 8192 + 4096 * d + 2048, [8, 128], BF16) for d in range(2)]
        tmpm = self.cv(Wb + 16384, [8, 128], F32)
        tmp1 = self.cv(Wb + 20480, [2048], F32)
        tmp2 = self.cv(164864, [2048], F32)
        for T in range(16):
            for d in range(2):
                for (src, dst) in ((X1[d], LB1[d]), (X2[d], LB2[d])):
                    t_m = self.tt("dve", tmpm, src[:, T * 128:(T + 1) * 128].unsqueeze(1).to_broadcast([128, 8, 128]), msk, ALU.mult, deps=[t_msk])
                    for h4 in range(2):
                        b, pb = self.bank()
                        for j in range(4):
                            tk = P.op("pe", lambda t, o=pb[:, j * 128:(j + 1) * 128], i=tmpm[:, h4 * 4 + j, :]: t.transpose(o, i, self.identf[:]),
                                      deps=[t_m, self.bank_free[b]], signal=(j == 3))
                        te = self.cp("act", dst[:, h4 * 4:(h4 + 1) * 4, :], pb.rearrange("p (a b) -> p a b", a=4), deps=[tk])
                        self.bank_free[b] = te
                    P.wait("dve", [tk])
                self.tt("dve", WP1[d], W1[d][:, T * 128:(T + 1) * 128].unsqueeze(1).to_broadcast([128, 8, 128]), msk, ALU.mult)
                self.tt("dve", WP2[d], W2[d][:, T * 128:(T + 1) * 128].unsqueeze(1).to_broadcast([128, 8, 128]), msk, ALU.mult)
            P.barrier()
            for s in range(self.nseq):
                t_u = P.dma("sp", uf, self.UT[s, T * 128:(T + 1) * 128, :], self.ld_ds)
                t_ub = self.cp("act", ub, uf, deps=[t_u])
                ybanks = [self.ps[:, i * 512:(i + 1) * 512] for i in range(4)]
                sb_i = 0
                pc_free = None
                for d in range(2):
                    for gl in range(8):
                        g = T * 8 + gl
                        first = (d == 0 and gl == 0)
                        last = (d == 1 and gl == 7)
                        t_tab = self.trig2(cosT, sinT, iof, thn[d][:, g:g + 1], tmp1, tmp2)
                        for tb in range(4):
                            b1 = 4 + sb_i
                            b2 = 5 + sb_i
                            sb_i = 2 - sb_i
                            pb1 = self.ps[:, b1 * 512:(b1 + 1) * 512]
                            pb2 = self.ps[:, b2 * 512:(b2 + 1) * 512]
                            self.mm(pb1, LB1[d][:, gl, :], ub[:, tb * NT:(tb + 1) * NT], True, True, deps=[t_ub, self.bank_free[b1]])
                            tk = self.mm(pb2, LB2[d][:, gl, :], ub[:, tb * NT:(tb + 1) * NT], True, True, deps=[self.bank_free[b2]], signal=True)
                            if d == 0:
                                sl = slice(tb * NT, (tb + 1) * NT)
                                i1, i2 = pb1, pb2
                            else:
                                sl = slice(S - (tb + 1) * NT, S - tb * NT)
                                i1, i2 = pb1[:, ::-1], pb2[:, ::-1]
                            self.tt("dve", St[:, sl], i1, cosT[:, sl], ALU.mult, deps=[tk, t_tab])
                            t_1 = self.tt("dve", tmp1[:, 0:NT], i2, sinT[:, sl], ALU.mult)
                            self.bank_free[b1] = t_1
                            self.bank_free[b2] = t_1
                            self.tt("dve", St[:, sl], St[:, sl], tmp1[:, 0:NT], ALU.add)
                        P.op("dve", lambda v, d=d, g=g: v.tensor_tensor_scan(out=Xt, data0=rho[d][:, g:g + 1].to_broadcast([128, S]), data1=St, initial=0.0,
                                                                           op0=ALU.mult, op1=ALU.add))
                        po = (lambda a: a) if d == 0 else (lambda a: a[:, ::-1])
                        self.tt("dve", po(Pc), Xt, cosT, ALU.mult, deps=[pc_free])
                        t_p = self.tt("dve", po(Ps), Xt, sinT, ALU.mult)
                        for tb in range(4):
                            self.mm(ybanks[tb], WP1[d][:, gl, :], Pc[:, tb * NT:(tb + 1) * NT], first, False, deps=[t_p])
                            pc_free = self.mm(ybanks[tb], WP2[d][:, gl, :], Ps[:, tb * NT:(tb + 1) * NT], False, last, signal=True)
                t_y = None
                for tb in range(4):
                    t_y = self.stt("dve", yt[:, tb * NT:(tb + 1) * NT], uf[:, tb * NT:(tb + 1) * NT], dvec[:, T:T + 1], ybanks[tb], ALU.mult, ALU.add, deps=[pc_free, t_d])
                t_g = self.act(yg, yt, AF.Gelu, deps=[t_y])
                P.dma("sp", self.HT[s, T * 128:(T + 1) * 128, :], yg, self.st_ds, deps=[t_g])
                P.barrier()
        self.bank_free = [None] * 8
        hb_free = None
        w_glu = self.iw["s5_w_glu"]
        self.tmp_free = [None] * 4
        for s in range(self.nseq):
            for tb in range(S // NT):
                t_x = self.load_x_block(s, tb * NT, deps=[self.xa_free])
                t_z = P.dma("sp", self.hB[:, :, 0:NT], self.xview(self.HT, s, tb * NT), self.ld_ds, deps=[hb_free])
                st = {}

                def ep_g(m, pb, tk):
                    j = m % 4
                    st[m] = self.act(self.tmpf[j][:], pb, AF.Sigmoid, deps=[tk, self.tmp_free[j]])
                    return st[m]

                def ep_a(m, pb, tk):
                    j = m % 4
                    t = self.tt("dve", self.tmpf[j][:], pb, self.tmpf[j][:], ALU.mult, deps=[tk, st[m]])
                    st["e"] = self.tt("dve", self.xA[:, m, :], self.xA[:, m, :], self.tmpf[j][:], ALU.add, deps=[t_x])
                    self.tmp_free[j] = st["e"]
                    return t
                for mg in range(4):
                    self.linear_block(w_glu, 512, D + mg * 512, self.hB[:, :, 0:NT], t_z, lambda m, pb, tk, mg=mg: ep_g(mg * 4 + m, pb, tk))
                    hb_free = self.linear_block(w_glu, 512, mg * 512, self.hB[:, :, 0:NT], t_z, lambda m, pb, tk, mg=mg: ep_a(mg * 4 + m, pb, tk))
                self.xa_free = self.store_x_block(s, tb * NT, deps=[st["e"]])
        P.barrier()
        self.sq_free = self.rstd_free = self.hb_free = self.c_free = None
        self.tmp_free = [None] * 4

    def build(self):
        P = self.P
        self.sq_free = None
        self.rstd_free = None
        self.xa_free = None
        self.hb_free = None
        self.c_free = None
        self.tmp_free = [None] * 4
        self.setup_consts()
        self.transpose_in()
        self.xa_free = self.xt_ready
        for sl in self.sublayers:
            P.barrier()
            if sl.startswith("mlp"):
                self.mlp(int(sl[3:]))
            else:
                getattr(self, sl)()
        P.barrier()
        self.transpose_out()
        P.wait("sp", [self.final_tok])
        P.emit()
        return self.nc


FULL = ["fnet", "mlp0", "conv", "mlp1", "na", "mlp2", "s5", "mlp3"]


def run(inputs, nseq=2, sublayers=None, ncores=NCORES, debug=False):
    if sublayers is None:
        sublayers = FULL
    kb = K(nseq, sublayers, debug)
    with kb.stack:
        nc = kb.build()
    host = {}
    for k, v in inputs.items():
        v = np.asarray(v)
        if k in ("mlp_w1", "mlp_w2"):
            for i in range(4):
                host["%s_%d" % (k, i)] = v[i]
        elif k in ("mix_norm", "mlp_norm", "x"):
            host[k] = v
        else:
            host[k] = v[0]
    in_maps = []
    for c in range(ncores):
        m = {"x": np.ascontiguousarray(host["x"][c * nseq:(c + 1) * nseq])}
        for n in kb.in_names:
            m[n] = np.ascontiguousarray(host[n])
        in_maps.append(m)
    res = run_bass_kernel_spmd(nc, in_maps, core_ids=list(range(ncores)))
    if debug:
        return res.results
    return np.concatenate([r["out"] for r in res.results], axis=0)


def kernel(**inputs):
    return run(inputs, nseq=2, sublayers=None).astype(np.float32)
```

```python
import numpy as np
import concourse.bass as bass
import concourse.mybir as mybir
from concourse.bass_utils import run_bass_kernel_spmd

F32 = mybir.dt.float32
BF16 = mybir.dt.bfloat16
I32 = mybir.dt.int32
AF = mybir.ActivationFunctionType
ALU = mybir.AluOpType

D = 2048
S = 2048
FF = 8192
NT = 512
KC = 16
NCORES = 8
MAGIC = 12582912.0
TWO_PI = 6.283185307179586


class Tok:
    __slots__ = ("sem", "val")

    def __init__(self, sem, val):
        self.sem = sem
        self.val = val


class DSem:
    def __init__(self, sem):
        self.sem = sem
        self.cnt = 0


class Eng:
    def __init__(self, name, sem):
        self.name = name
        self.sem = sem
        self.cnt = 0
        self.ops = []
        self.waited = {}


class Prog:
    def __init__(self, nc, stack):
        self.nc = nc
        self.stack = stack
        self.eng = {}
        for n in ("sp", "act", "pool", "dve", "pe"):
            self.eng[n] = Eng(n, stack.enter_context(nc.semaphore("sem_" + n)))
        self.nds = 0
        self.all_ds = []

    def dsem(self):
        self.nds += 1
        d = DSem(self.stack.enter_context(self.nc.semaphore("ds%d" % self.nds)))
        self.all_ds.append(d)
        return d

    def barrier(self):
        toks = [Tok(e.sem, e.cnt) for e in self.eng.values() if e.cnt > 0]
        toks += [Tok(d.sem, d.cnt) for d in self.all_ds if d.cnt > 0]
        for en in self.eng:
            self.wait(en, toks)

    def _waits(self, e, deps):
        w = []
        for t in deps:
            if t is None:
                continue
            k = id(t.sem)
            if e.waited.get(k, 0) < t.val:
                e.waited[k] = t.val
                w.append((t.sem, t.val))
        return w

    def op(self, en, fn, deps=(), serial=True, signal=True):
        e = self.eng[en]
        deps = list(deps)
        if serial and en in ("act", "dve", "pool") and e.cnt > 0:
            deps.append(Tok(e.sem, e.cnt))
        w = self._waits(e, deps)
        if signal:
            e.cnt += 1
            tok = Tok(e.sem, e.cnt)
            inc = (e.sem, 1)
        else:
            tok = None
            inc = None
        e.ops.append((w, fn, inc))
        return tok

    def dma(self, qn, out, in_, ds, deps=(), slow=False):
        e = self.eng[qn]
        w = self._waits(e, deps)
        ds.cnt += 16
        if slow:
            e.ops.append((w, (lambda q, o=out, i=in_: q.dma_start(out=o, in_=i, allow_slow_non_contiguous=True)), (ds.sem, 16)))
        else:
            e.ops.append((w, (lambda q, o=out, i=in_: q.dma_start(out=o, in_=i)), (ds.sem, 16)))
        return Tok(ds.sem, ds.cnt)

    def wait(self, en, deps):
        e = self.eng[en]
        w = self._waits(e, deps)
        if w:
            e.ops.append((w, None, None))

    def emit(self):
        nc = self.nc
        with nc.Block() as block:
            def mk(e):
                def run(q):
                    for w, fn, inc in e.ops:
                        for sem, val in w:
                            q.wait_ge(sem, val)
                        if fn is not None:
                            ins = fn(q)
                            if inc is not None:
                                ins.then_inc(inc[0], inc[1])
                return run
            block.sync(mk(self.eng["sp"]))
            block.scalar(mk(self.eng["act"]))
            block.gpsimd(mk(self.eng["pool"]))
            block.vector(mk(self.eng["dve"]))
            block.tensor(mk(self.eng["pe"]))


ALL_INPUTS = [
    ("mix_norm", [4, D]), ("fnet_w", [D, D]), ("fnet_b", [D]),
    ("conv_w_in", [D, 2 * D]), ("conv_b_in", [2 * D]), ("conv_w_dw", [31, D]),
    ("conv_b_dw", [D]), ("conv_ln_g", [D]), ("conv_ln_b", [D]), ("conv_w_out", [D, D]),
    ("conv_b_out", [D]), ("na_w_qkv", [D, 3 * D]), ("na_q_gain", [128]), ("na_k_gain", [128]),
    ("na_rpb", [16, 15, 31]), ("na_w_o", [D, D]), ("s5_w_in", [D, D]),
    ("s5_a_re", [2, 128, 64]), ("s5_a_im", [2, 128, 64]), ("s5_log_dt", [2, 128]),
    ("s5_b_re", [2, 128, 64, 16]), ("s5_b_im", [2, 128, 64, 16]),
    ("s5_c_re", [2, 128, 16, 64]), ("s5_c_im", [2, 128, 16, 64]), ("s5_d", [128, 16]),
    ("s5_w_glu", [D, 2 * D]), ("mlp_norm", [4, D]),
    ("mlp_w1_0", [D, FF]), ("mlp_w2_0", [FF, D]), ("mlp_w1_1", [D, FF]), ("mlp_w2_1", [FF, D]),
    ("mlp_w1_2", [D, FF]), ("mlp_w2_2", [FF, D]), ("mlp_w1_3", [D, FF]), ("mlp_w2_3", [FF, D]),
]
PREFIX = {"fnet": "fnet_", "conv": "conv_", "na": "na_", "s5": "s5_"}


def needed_inputs(sublayers):
    names = ["mix_norm", "mlp_norm"]
    for sl in sublayers:
        if sl.startswith("mlp"):
            names += ["mlp_w1_" + sl[3:], "mlp_w2_" + sl[3:]]
        else:
            names += [n for n, _ in ALL_INPUTS if n.startswith(PREFIX[sl])]
    return [(n, sh) for n, sh in ALL_INPUTS if n in names]


class K:
    def __init__(self, nseq, sublayers, debug=False):
        import contextlib
        self.nseq = nseq
        self.sublayers = sublayers
        self.stack = contextlib.ExitStack()
        nc = self.nc = bass.Bass("TRN2", target_bir_lowering=False)
        st = self.stack
        self.P = Prog(nc, st)
        dt = lambda name, shape, kind, ty=F32: nc.dram_tensor(name, list(shape), ty, kind=kind).ap()
        I = "ExternalInput"
        self.x = dt("x", [nseq, S, D], I)
        self.out = dt("out", [nseq, S, D], "ExternalOutput")
        self.iw = {}
        self.in_names = []
        for name, shape in needed_inputs(sublayers):
            self.iw[name] = dt(name, shape, I)
            self.in_names.append(name)
        SK = "ExternalOutput" if debug else "Internal"
        self.XT = dt("XT", [nseq, D, S], SK)
        self.HT = dt("HTs", [nseq, D, S], SK, BF16)
        self.ZT = dt("ZTs", [nseq, D, S], SK, BF16)
        self.UT = dt("UTs", [nseq, D, S], SK, F32)
        self.CSd = dt("CSd", [S, S], SK, BF16)
        self.SSd = dt("SSd", [S, S], SK, BF16)
        self.RB = dt("RBs", [16, 15, 128], SK, F32)
        self.w1b = {}
        for sl_ in sublayers:
            if sl_.startswith("mlp"):
                self.w1b[int(sl_[3:])] = dt("w1b_%s" % sl_[3:], [D, FF], "Internal", BF16)
        self.conv_ds = self.P.dsem()
        self.conv_tok = {}
        sb = lambda name, shape, ty: st.enter_context(nc.sbuf_tensor(name, list(shape), ty))
        self.arena = sb("arena", [128, 98304], BF16)
        self.xA = self.cv(0, [KC, NT], F32)
        self.hB = self.cv(32768, [KC, NT + 32], BF16)
        self.sqb = self.cv(50176, [KC, NT], BF16)
        self.C = self.cv(66560, [64, NT], BF16)
        self.wr = [self.cv(132096, [8192], BF16), self.cv(148480, [8192], BF16)]
        self.tmpf = [self.cv(164864 + 2048 * i, [NT], F32) for i in range(4)]
        self.rstd = self.cv(173056, [NT], F32)
        self.MISC = 175104
        self.ones = sb("ones", [128, 128], BF16)
        self.identf = sb("identf", [128, 128], F32)
        self.identb = sb("identb", [128, 128], BF16)
        self.iot = sb("iot", [128, 128], I32)
        self.gvec = sb("gvec", [128, 8, KC], F32)
        self.epsc = sb("epsc", [128, 1], F32)
        self.pidx = sb("pidx", [128, 1], F32)
        self.ps = st.enter_context(nc.psum_tensor("ps", [128, 8 * 512], F32))
        self.bank_free = [None] * 8
        self.bank_i = 0
        self.wr_ds = [self.P.dsem() for _ in range(2)]
        self.wr_free = [None, None]
        self.wr_i = 0
        self.ld_ds = self.P.dsem()
        self.st_ds = self.P.dsem()
        self.c_ds = self.P.dsem()
        self.last_store = {}

    def cv(self, off, shape, ty):
        n = 1
        for d_ in shape:
            n *= d_
        nb = n * (4 if ty in (F32, I32) else 2)
        a = self.arena[:, off // 2:(off + nb) // 2]
        if ty != BF16:
            a = a.bitcast(ty)
        if len(shape) == 2:
            a = a.rearrange("p (a b) -> p a b", a=shape[0])
        elif len(shape) == 3:
            a = a.rearrange("p (a b c) -> p a b c", a=shape[0], b=shape[1])
        return a

    def bank(self):
        b = self.bank_i
        self.bank_i = (b + 1) % 8
        return b, self.ps[:, b * 512:(b + 1) * 512]

    def wslot(self):
        i = self.wr_i
        self.wr_i = 1 - i
        return i

    def tt(self, en, out, a, b, op, deps=(), serial=True):
        return self.P.op(en, lambda e: e.tensor_tensor(out=out, in0=a, in1=b, op=op), deps, serial)

    def ts(self, en, out, a, s1, op0, s2=None, op1=None, deps=(), serial=True):
        if op1 is None:
            return self.P.op(en, lambda e: e.tensor_scalar(out=out, in0=a, scalar1=s1, scalar2=None, op0=op0), deps, serial)
        return self.P.op(en, lambda e: e.tensor_scalar(out=out, in0=a, scalar1=s1, scalar2=s2, op0=op0, op1=op1), deps, serial)

    def stt(self, en, out, a, sc, b, op0, op1, deps=(), serial=True):
        return self.P.op(en, lambda e: e.scalar_tensor_tensor(out=out, in0=a, scalar=sc, in1=b, op0=op0, op1=op1), deps, serial)

    def act(self, out, in_, func, bias=None, scale=None, deps=(), serial=True):
        kw = {}
        if bias is not None:
            kw["bias"] = bias
        if scale is not None:
            kw["scale"] = scale
        return self.P.op("act", lambda a: a.activation(out=out, in_=in_, func=func, **kw), deps, serial)

    def cp(self, en, out, in_, deps=(), serial=True):
        if en == "act":
            return self.P.op(en, lambda e: e.activation(out=out, in_=in_, func=AF.Identity), deps, serial)
        return self.P.op(en, lambda e: e.tensor_copy(out=out, in_=in_), deps, serial)

    def mm(self, out, lhsT, rhs, start, stop, deps=(), signal=False):
        return self.P.op("pe", lambda t: t.matmul(out, lhsT=lhsT, rhs=rhs, start=start, stop=stop), deps, signal=signal)

    def setup_consts(self):
        P = self.P
        P.op("pool", lambda g: g.memset(self.ones[:], 1.0))
        P.op("pool", lambda g: g.memset(self.epsc[:], 1e-6))
        P.op("pool", lambda g: g.iota(self.iot[:], pattern=[[1, 128]], base=0, channel_multiplier=-1))
        P.op("pool", lambda g: g.tensor_copy(out=self.identf[:], in_=self.iot[:]))
        P.op("pool", lambda g: g.tensor_single_scalar(out=self.identf[:], in_=self.identf[:], scalar=0.0, op=ALU.is_equal))
        P.op("pool", lambda g: g.tensor_copy(out=self.identb[:], in_=self.identf[:]))
        P.op("pool", lambda g: g.iota(self.iot[:, 0:1], pattern=[[0, 1]], base=0, channel_multiplier=1))
        self.t_const = P.op("pool", lambda g: g.tensor_copy(out=self.pidx[:], in_=self.iot[:, 0:1]))
        toks = []
        for j, nm in enumerate(("mix_norm", "mlp_norm")):
            for i in range(4):
                src = self.iw[nm][i, :].rearrange("(kc p) -> p kc", p=128)
                toks.append(P.dma("sp", self.gvec[:, j * 4 + i, :], src, self.c_ds, slow=True))
        self.t_gvec = toks[-1]

    def load_vec(self, dst, src1d):
        return self.P.dma("sp", dst, src1d.rearrange("(kc p) -> p kc", p=128), self.c_ds, slow=True)

    def iota_free(self, dst_f32, n, tmp_i32):
        P = self.P
        for c0 in range(0, n, 512):
            c1 = min(n, c0 + 512)
            P.op("pool", lambda g, c0=c0, c1=c1: g.iota(tmp_i32[:, c0:c1], pattern=[[1, c1 - c0]], base=c0, channel_multiplier=0))
        return P.op("pool", lambda g: g.tensor_copy(out=dst_f32, in_=tmp_i32))

    def trig(self, out, idx, scal, inv, shift, tmp1, tmp2, scale_out=None, deps=()):
        self.ts("dve", tmp1, idx, scal, ALU.mult, inv, ALU.mult, deps=list(deps) + [getattr(self, "trig_free", None)])
        if shift != 0.0:
            self.ts("dve", tmp1, tmp1, shift, ALU.add)
        self.ts("dve", tmp2, tmp1, MAGIC, ALU.add)
        self.ts("dve", tmp2, tmp2, MAGIC, ALU.subtract)
        t = self.tt("dve", tmp1, tmp1, tmp2, ALU.subtract)
        if scale_out is None:
            self.trig_free = self.act(out, tmp1, AF.Sin, scale=TWO_PI, deps=[t])
            return self.trig_free
        t = self.act(tmp2, tmp1, AF.Sin, scale=TWO_PI, deps=[t])
        self.trig_free = t
        return self.ts("dve", out, tmp2, scale_out, ALU.mult, deps=[t])

    def trig2(self, cos_out, sin_out, idx, scal, tmp1, tmp2):
        self.ts("dve", tmp1, idx, scal, ALU.mult, deps=[getattr(self, "trig_free", None)])
        self.ts("dve", tmp2, tmp1, MAGIC, ALU.add)
        self.ts("dve", tmp2, tmp2, MAGIC, ALU.subtract)
        t = self.tt("dve", tmp1, tmp1, tmp2, ALU.subtract)
        self.act(sin_out, tmp1, AF.Sin, scale=TWO_PI, deps=[t])
        self.ts("dve", tmp2, tmp1, 0.25, ALU.is_gt)
        t = self.stt("dve", tmp2, tmp1, 0.25, tmp2, ALU.add, ALU.subtract)
        self.trig_free = self.act(cos_out, tmp2, AF.Sin, scale=TWO_PI, deps=[t])
        return self.trig_free

    def transpose_in(self):
        P = self.P
        t_ev = None
        prev_st = None
        for s in range(self.nseq):
            for tt in range(S // 128):
                buf = self.xA[:, 0:4, :].rearrange("p a b -> p (a b)")
                t_ld = P.dma("sp", buf, self.x[s, tt * 128:(tt + 1) * 128, :], self.ld_ds, deps=[t_ev])
                obuf = self.xA[:, 4:8, :].rearrange("p a b -> p (a b)").rearrange("p (k t) -> p k t", k=KC)
                for q4 in range(4):
                    b, pb = self.bank()
                    for j in range(4):
                        k = q4 * 4 + j
                        tk = P.op("pe", lambda t, o=pb[:, j * 128:(j + 1) * 128], i=buf[:, k * 128:(k + 1) * 128]: t.transpose(o, i, self.identf[:]),
                                  deps=[t_ld, self.t_const, self.bank_free[b]], signal=(j == 3))
                    t_ev = self.cp("dve", obuf[:, q4 * 4:(q4 + 1) * 4, :], pb.rearrange("p (k t) -> p k t", k=4), deps=[tk, prev_st], serial=False)
                    self.bank_free[b] = t_ev
                dst = self.XT[s].rearrange("(kc p) t -> p kc t", p=128)[:, :, tt * 128:(tt + 1) * 128]
                prev_st = P.dma("sp", dst, obuf, self.st_ds, deps=[t_ev])
        self.xt_ready = prev_st

    def transpose_out(self):
        P = self.P
        t_ev = None
        last = None
        for s in range(self.nseq):
            for tt in range(S // 128):
                buf = self.xA[:, 0:4, :].rearrange("p a b -> p (a b)").rearrange("p (k t) -> p k t", k=KC)
                src = self.XT[s].rearrange("(kc p) t -> p kc t", p=128)[:, :, tt * 128:(tt + 1) * 128]
                t_ld = P.dma("sp", buf, src, self.ld_ds, deps=[t_ev, self.xt_ready])
                obuf = self.xA[:, 4:8, :].rearrange("p a b -> p (a b)")
                for q4 in range(4):
                    b, pb = self.bank()
                    for j in range(4):
                        k = q4 * 4 + j
                        tk = P.op("pe", lambda t, o=pb[:, j * 128:(j + 1) * 128], i=buf[:, k, :]: t.transpose(o, i, self.identf[:]),
                                  deps=[t_ld, self.t_const, self.bank_free[b]], signal=(j == 3))
                    t_ev = self.cp("dve", obuf[:, q4 * 512:(q4 + 1) * 512], pb, deps=[tk, last], serial=False)
                    self.bank_free[b] = t_ev
                last = P.dma("sp", self.out[s, tt * 128:(tt + 1) * 128, :], obuf, self.st_ds, deps=[t_ev])
        self.final_tok = last

    def xview(self, T, s, t0, n=NT):
        return T[s].rearrange("(kc p) t -> p kc t", p=128)[:, :, t0:t0 + n]

    def load_x_block(self, s, t0, deps=()):
        return self.P.dma("sp", self.xA[:], self.xview(self.XT, s, t0), self.ld_ds, deps=list(deps) + [self.xt_ready])

    def store_x_block(self, s, t0, deps=()):
        t = self.P.dma("sp", self.xview(self.XT, s, t0), self.xA[:], self.st_ds, deps=list(deps))
        self.xt_ready = t
        return t

    def rmsnorm_block(self, t_x, gidx, out3, deps_out=()):
        P = self.P
        t_sq = self.act(self.sqb[:], self.xA[:], AF.Square, deps=[t_x, self.sq_free])
        b, pb = self.bank()
        for k in range(KC):
            tk = self.mm(pb, self.ones[:], self.sqb[:, k, :], k == 0, k == KC - 1, deps=[t_sq, self.t_const, self.bank_free[b]], signal=(k == KC - 1))
        self.sq_free = tk
        t1 = self.act(self.rstd[:], pb, AF.Sqrt, bias=self.epsc[:, 0:1], scale=1.0 / D, deps=[tk, self.rstd_free])
        self.bank_free[b] = t1
        t2 = P.op("dve", lambda v: v.reciprocal(out=self.rstd[:], in_=self.rstd[:]), deps=[t1])
        t3 = None
        for k in range(KC):
            t3 = self.stt("dve", out3[:, k, :], self.xA[:, k, :], self.gvec[:, gidx, k:k + 1], self.rstd[:], ALU.mult, ALU.mult,
                          deps=[t2, self.t_gvec] + list(deps_out), serial=(k == 0))
        self.rstd_free = t3
        return t3

    def load_w(self, src3, shape3, deps=(), q="pool"):
        i = self.wslot()
        a, bsz = shape3
        view = self.wr[i][:, 0:a * bsz].rearrange("p (a b) -> p a b", a=a)
        t = self.P.dma(q, view, src3, self.wr_ds[i], deps=[self.wr_free[i]] + list(deps))
        return view, t, i

    def preconvert(self, li):
        if li in self.conv_tok:
            return
        w1 = self.iw["mlp_w1_%d" % li]
        tok = None
        for r in range(0, D, 128):
            tok = self.P.dma("pool", self.w1b[li][r:r + 128, :], w1[r:r + 128, :], self.conv_ds)
        self.conv_tok[li] = tok

    def linear_block(self, w_dram, ncols, col0, src3, t_src, epilogue, kc=KC, q="pool", wdeps=()):
        wv3 = w_dram.rearrange("(kc p) f -> p kc f", p=128)
        nm = ncols // 128
        g = 4 if kc == 16 else 1
        tk = None
        for mg in range(0, nm, g):
            gg = min(g, nm - mg)
            wv, t_w, wi = self.load_w(wv3[:, :, col0 + mg * 128: col0 + (mg + gg) * 128], (kc, gg * 128), deps=wdeps, q=q)
            for mt in range(gg):
                b, pb = self.bank()
                for k in range(kc):
                    tk = self.mm(pb, wv[:, k, mt * 128:(mt + 1) * 128], src3[:, k, :], k == 0, k == kc - 1,
                                 deps=[t_w, t_src, self.bank_free[b]], signal=(k == kc - 1))
                self.bank_free[b] = epilogue(mg + mt, pb, tk)
            self.wr_free[wi] = tk
        return tk

    def mlp(self, li):
        self.preconvert(li)
        w1 = self.iw["mlp_w1_%d" % li]
        w2 = self.iw["mlp_w2_%d" % li]
        for s in range(self.nseq):
            for tb in range(S // NT):
                t0 = tb * NT
                t_x = self.load_x_block(s, t0, deps=[self.xa_free])
                t_h = self.rmsnorm_block(t_x, 4 + li, self.hB[:, :, 0:NT], deps_out=[self.hb_free])
                st = {}

                def ep1(m, pb, tk):
                    j = m % 4
                    tr = self.act(self.tmpf[j][:], pb, AF.Relu, deps=[tk, self.tmp_free[j]])
                    st["c"] = self.tt("dve", self.C[:, m, :], self.tmpf[j][:], self.tmpf[j][:], ALU.mult, deps=[tr, self.c_free], serial=False)
                    self.tmp_free[j] = st["c"]
                    return tr
                tk = self.linear_block(self.w1b[li], FF, 0, self.hB[:, :, 0:NT], t_h, ep1, q="sp", wdeps=[self.conv_tok[li]])
                self.hb_free = tk

                def ep2(m, pb, tk):
                    st["e"] = self.tt("dve", self.xA[:, m, :], pb, self.xA[:, m, :], ALU.add, deps=[tk, t_h], serial=False)
                    return st["e"]
                tk = self.linear_block(w2, D, 0, self.C, st["c"], ep2, kc=64)
                self.c_free = tk
                self.xa_free = self.store_x_block(s, t0, deps=[st["e"]])

    def fnet(self):
        P = self.P
        M = self.MISC
        ccsc = self.cv(M, [2, 512], BF16)
        bvec = self.cv(M + 2048, [KC], F32)
        t_b = self.load_vec(bvec, self.iw["fnet_b"])
        iof = self.cv(0, [2048], F32)
        tmp1 = self.cv(8192, [2048], F32)
        tmp2 = self.cv(16384, [2048], F32)
        tmpi = self.cv(24576, [2048], I32)
        svec = self.cv(M + 2112, [1], F32)
        t_io = self.iota_free(iof, 2048, tmpi)
        NORM = 1.0 / (16.0 * (2048.0 ** 0.5))
        for kc in range(2):
            self.ts("dve", svec, self.pidx[:], float(kc * 128), ALU.add, deps=[t_io, self.t_const])
            self.trig(ccsc[:, kc, 0:256], iof[:, 0:256], svec[:, 0:1], 1.0 / 256, 0.25, tmp1[:, 0:256], tmp2[:, 0:256], scale_out=NORM)
            t_cc = self.trig(ccsc[:, kc, 256:512], iof[:, 0:256], svec[:, 0:1], 1.0 / 256, 0.5, tmp1[:, 0:256], tmp2[:, 0:256], scale_out=NORM)
        ob = [self.cv(50176, [2048], BF16), self.cv(50176 + 4096, [2048], BF16)]
        t_st = [None, None]
        for tt in range(16):
            self.ts("dve", svec, self.pidx[:], float(tt * 128), ALU.add)
            for j, (dst, shift) in enumerate(((self.CSd, 0.25), (self.SSd, 0.0))):
                P.wait("dve", [t_st[j]])
                P.wait("act", [t_st[j]])
                t = self.trig(ob[j], iof, svec[:, 0:1], 1.0 / 2048, shift, tmp1, tmp2)
                t_st[j] = P.dma("sp", dst[tt * 128:(tt + 1) * 128, :], ob[j], self.st_ds, deps=[t])
        P.barrier()
        self.sq_free = self.rstd_free = None
        t_stp = None
        for s in range(self.nseq):
            for tb in range(S // NT):
                t_x = self.load_x_block(s, tb * NT, deps=[self.xa_free])
                t_h = self.rmsnorm_block(t_x, 0, self.hB[:, :, 0:NT], deps_out=[t_stp])
                self.xa_free = t_h
                t_stp = P.dma("sp", self.xview(self.HT, s, tb * NT), self.hB[:, :, 0:NT], self.st_ds, deps=[t_h])
        P.barrier()
        hg = self.C[:, 0:8, :].rearrange("p a b -> p (a b)").rearrange("p (k t) -> p k t", k=2)
        pq = self.hB[:, :, 0:512]
        zst = [self.cv(0, [2, 512], BF16), self.cv(2048, [2, 512], BF16)]
        zst_free = [None, None]
        hg_free = None
        pq_free = None
        zi = 0
        csv = self.CSd.rearrange("(tt p) s -> p tt s", p=128)
        ssv = self.SSd.rearrange("(tt p) s -> p tt s", p=128)
        for s in range(self.nseq):
            for g in range(8):
                t_hg = P.dma("sp", hg, self.HT[s, g * 256:(g + 1) * 256, :].rearrange("(k p) t -> p k t", p=128), self.ld_ds, deps=[hg_free])
                t_pq = None
                for tt in range(16):
                    b, pb = self.bank()
                    for kc in range(2):
                        tk = self.mm(pb, hg[:, kc, tt * 128:(tt + 1) * 128], ccsc[:, kc, :], kc == 0, kc == 1,
                                     deps=[t_hg, t_cc, self.bank_free[b]], signal=(kc == 1))
                    t_pq = self.cp("act" if tt % 2 else "dve", pq[:, tt, :], pb, deps=[tk, pq_free], serial=False)
                    self.bank_free[b] = t_pq
                t_pq2 = Tok(P.eng["dve"].sem, P.eng["dve"].cnt)
                t_pq3 = Tok(P.eng["act"].sem, P.eng["act"].cnt)
                hg_free = tk
                for sbk in range(4):
                    cv_, t_c, ci = self.load_w(csv[:, :, sbk * 512:(sbk + 1) * 512], (16, 512), q="sp")
                    sv_, t_s, si = self.load_w(ssv[:, :, sbk * 512:(sbk + 1) * 512], (16, 512), q="act")
                    zb = zst[zi]
                    for ct in range(2):
                        b, pb = self.bank()
                        for tt in range(16):
                            self.mm(pb, pq[:, tt, ct * 128:(ct + 1) * 128], cv_[:, tt, :], tt == 0, False,
                                    deps=[t_c, t_pq2, t_pq3, self.bank_free[b]])
                        for tt in range(16):
                            tk = self.mm(pb, pq[:, tt, 256 + ct * 128:256 + (ct + 1) * 128], sv_[:, tt, :], False, tt == 15,
                                         deps=[t_s], signal=(tt == 15))
                        t_z = self.cp("act", zb[:, ct, :], pb, deps=[tk, zst_free[zi]])
                        self.bank_free[b] = t_z
                    self.wr_free[ci] = tk
                    self.wr_free[si] = tk
                    zst_free[zi] = P.dma("sp", self.ZT[s, g * 256:(g + 1) * 256, sbk * 512:(sbk + 1) * 512].rearrange("(k p) t -> p k t", p=128),
                                         zb, self.st_ds, deps=[t_z])
                    zi = 1 - zi
                pq_free = tk
        P.barrier()
        hb_free = None
        for s in range(self.nseq):
            for tb in range(S // NT):
                t_x = self.load_x_block(s, tb * NT, deps=[self.xa_free])
                t_z = P.dma("sp", self.hB[:, :, 0:NT], self.xview(self.ZT, s, tb * NT), self.ld_ds, deps=[hb_free])
                st = {}

                def ep(m, pb, tk):
                    st["e"] = self.stt("dve", self.xA[:, m, :], pb, bvec[:, m:m + 1], self.xA[:, m, :], ALU.add, ALU.add, deps=[tk, t_x, t_b], serial=False)
                    return st["e"]
                hb_free = self.linear_block(self.iw["fnet_w"], D, 0, self.hB[:, :, 0:NT], t_z, ep)
                self.xa_free = self.store_x_block(s, tb * NT, deps=[st["e"]])
        P.barrier()
        self.sq_free = self.rstd_free = self.hb_free = self.c_free = None

    def conv(self):
        P = self.P
        M = self.MISC
        bin_ = self.cv(M, [32], F32)
        bdw = self.cv(M + 128, [KC], F32)
        lng = self.cv(M + 192, [KC], F32)
        lnb = self.cv(M + 256, [KC], F32)
        bout = self.cv(M + 320, [KC], F32)
        wdw = self.cv(M + 384, [KC, 31], F32)
        dg = self.cv(M + 384 + 1984, [31, 128], BF16)
        mean = self.cv(M + 10304, [NT], F32)
        nmr = self.cv(M + 12352, [NT], F32)
        rs2 = self.cv(M + 14400, [NT], F32)
        self.load_vec(bin_, self.iw["conv_b_in"])
        self.load_vec(bdw, self.iw["conv_b_dw"])
        self.load_vec(lng, self.iw["conv_ln_g"])
        self.load_vec(lnb, self.iw["conv_ln_b"])
        self.load_vec(bout, self.iw["conv_b_out"])
        for m_ in range(16):
            t_p = P.dma("sp", wdw[:, m_, :], self.iw["conv_w_dw"][:, m_ * 128:(m_ + 1) * 128].rearrange("k p -> p k"), self.c_ds, slow=True)
        U = self.C[:, 0:16, :]
        u_free = None
        w_in = self.iw["conv_w_in"]
        for s in range(self.nseq):
            for tb in range(S // NT):
                t_x = self.load_x_block(s, tb * NT, deps=[self.xa_free])
                t_h = self.rmsnorm_block(t_x, 1, self.hB[:, :, 0:NT], deps_out=[self.hb_free])
                self.xa_free = t_h
                st = {}

                def ep_g(m, pb, tk):
                    j = m % 4
                    st[m] = self.act(self.tmpf[j][:], pb, AF.Sigmoid, bias=bin_[:, 16 + m:17 + m], deps=[tk, self.tmp_free[j], t_p])
                    return st[m]

                def ep_a(m, pb, tk):
                    j = m % 4
                    st["u"] = self.stt("dve", U[:, m, :], pb, bin_[:, m:m + 1], self.tmpf[j][:], ALU.add, ALU.mult, deps=[tk, st[m], u_free], serial=False)
                    self.tmp_free[j] = st["u"]
                    return st["u"]
                for mg in range(4):
                    self.linear_block(w_in, 512, D + mg * 512, self.hB[:, :, 0:NT], t_h, lambda m, pb, tk, mg=mg: ep_g(mg * 4 + m, pb, tk))
                    self.hb_free = self.linear_block(w_in, 512, mg * 512, self.hB[:, :, 0:NT], t_h, lambda m, pb, tk, mg=mg: ep_a(mg * 4 + m, pb, tk))
                u_free = P.dma("sp", self.xview(self.HT, s, tb * NT), U, self.st_ds, deps=[st["u"]])
        P.barrier()
        self.sq_free = self.rstd_free = self.hb_free = None
        V = self.C[:, 0:32, :].bitcast(F32).rearrange("p a b -> p (a b)").rearrange("p (m t) -> p m t", m=16)
        hb_free = None
        v_free = None
        dg_free = None
        w_out = self.iw["conv_w_out"]
        for s in range(self.nseq):
            for tb in range(S // NT):
                t0 = tb * NT
                lo = max(t0 - 15, 0)
                hi = min(t0 + NT + 15, S)
                off = lo - (t0 - 15)
                deps = [hb_free]
                if t0 == 0:
                    deps.append(P.op("pool", lambda g: g.memset(self.hB[:, :, 0:15], 0.0), deps=[hb_free]))
                if t0 + NT == S:
                    deps.append(P.op("pool", lambda g: g.memset(self.hB[:, :, NT + 15:NT + 30], 0.0), deps=[hb_free]))
                t_u = P.dma("sp", self.hB[:, :, off:off + hi - lo], self.xview(self.HT, s, lo, hi - lo), self.ld_ds, deps=deps)
                t_x = self.load_x_block(s, t0, deps=[self.xa_free])
                t_v = None
                for m in range(16):
                    for k in range(31):
                        t_d = self.ts("dve", dg[:, k, :], self.identb[:], wdw[:, m, k:k + 1], ALU.mult, deps=[dg_free, t_p, self.t_const], serial=False)
                    b, pb = self.bank()
                    for k in range(31):
                        tk = self.mm(pb, dg[:, k, :], self.hB[:, m, k:k + NT], k == 0, k == 30, deps=[t_d, t_u, self.bank_free[b]], signal=(k == 30))
                    dg_free = tk
                    t_v = self.act(V[:, m, :], pb, AF.Identity, bias=bdw[:, m:m + 1], deps=[tk, v_free])
                    self.bank_free[b] = t_v
                hb_free = tk
                t_sq = self.act(self.sqb[:], V, AF.Square, deps=[t_v, self.sq_free])
                t_vb = self.cp("dve", self.hB[:, :, 0:NT], V, deps=[t_v, tk])
                b1, pb1 = self.bank()
                b2, pb2 = self.bank()
                for m in range(16):
                    tk1 = self.mm(pb1, self.ones[:], self.hB[:, m, 0:NT], m == 0, m == 15, deps=[t_vb, self.bank_free[b1]], signal=(m == 15))
                for m in range(16):
                    tk2 = self.mm(pb2, self.ones[:], self.sqb[:, m, :], m == 0, m == 15, deps=[t_sq, self.bank_free[b2]], signal=(m == 15))
                self.sq_free = tk2
                t_m = self.act(mean, pb1, AF.Identity, scale=1.0 / D, deps=[tk1])
                self.bank_free[b1] = t_m
                self.tt("dve", nmr, mean, mean, ALU.mult, deps=[t_m])
                t_var = self.stt("dve", rs2, pb2, 1.0 / D, nmr, ALU.mult, ALU.subtract, deps=[tk2])
                self.bank_free[b2] = t_var
                t_sd = self.act(rs2, rs2, AF.Sqrt, bias=self.epsc[:, 0:1], deps=[t_var])
                P.op("dve", lambda v: v.reciprocal(out=rs2, in_=rs2), deps=[t_sd])
                self.tt("dve", nmr, mean, rs2, ALU.mult)
                t_n = self.ts("dve", nmr, nmr, -1.0, ALU.mult)
                t_s = None
                for m in range(16):
                    self.tt("dve", V[:, m, :], V[:, m, :], rs2, ALU.mult, deps=[t_n])
                    t_a = self.tt("dve", V[:, m, :], V[:, m, :], nmr, ALU.add)
                    t_s = self.act(self.hB[:, m, 0:NT], V[:, m, :], AF.Silu, bias=lnb[:, m:m + 1], scale=lng[:, m:m + 1], deps=[t_a, tk1])
                v_free = t_s
                st = {}

                def ep(m, pb, tk):
                    st["e"] = self.stt("dve", self.xA[:, m, :], pb, bout[:, m:m + 1], self.xA[:, m, :], ALU.add, ALU.add, deps=[tk, t_x], serial=False)
                    return st["e"]
                hb_free = self.linear_block(w_out, D, 0, self.hB[:, :, 0:NT], t_s, ep)
                self.xa_free = self.store_x_block(s, t0, deps=[st["e"]])
        P.barrier()
        self.sq_free = self.rstd_free = self.hb_free = self.c_free = None
        self.tmp_free = [None] * 4

    def na(self):
        P = self.P
        nc = self.nc
        M = self.MISC
        gq = self.cv(M, [1], F32)
        gk = self.cv(M + 4, [1], F32)
        mk = self.cv(M + 64, [64], F32)
        kv = self.cv(M + 320, [1], F32)
        t1 = self.cv(M + 384, [64], F32)
        t2 = self.cv(M + 640, [64], F32)
        ti = self.cv(M + 896, [64], I32)
        toep = self.cv(M + 1280, [15, 64], F32)
        rcp = self.cv(M + 5120, [64], F32)
        zer = self.cv(M + 5376, [1920], F32)
        ebuf = self.cv(M + 13056, [8, 64], BF16)
        sbuf_ = self.cv(M + 14080, [8, 64], F32)
        sqh = self.cv(M + 16128, [NT], BF16)
        rsq = self.cv(M + 17152, [NT], F32)
        toep_tmp = self.cv(M + 5376, [15, 64], F32)
        wq3 = self.cv(0, [16, 3, 128], BF16)
        qT = self.cv(12288, [S], BF16)
        kT = self.cv(16384, [S], BF16)
        vrow = self.cv(20480, [32, 128], BF16)
        oT = self.cv(28672, [S], BF16)
        hT = self.C[:].rearrange("p a b -> p (a b)").rearrange("p (k t) -> p k t", k=16)
        t_g = P.dma("sp", gq, self.iw["na_q_gain"].rearrange("(p o) -> p o", o=1), self.c_ds, slow=True)
        t_g = P.dma("sp", gk, self.iw["na_k_gain"].rearrange("(p o) -> p o", o=1), self.c_ds, slow=True)
        t_gq = self.ts("dve", gq, gq, 128.0 ** -0.5, ALU.mult, deps=[t_g])
        P.op("pool", lambda g: g.memset(zer, 0.0))
        tz = Tok(P.eng["pool"].sem, P.eng["pool"].cnt)
        t_rb = P.dma("sp", self.RB.rearrange("h r c -> h (r c)"), zer[0:16, :], self.st_ds, deps=[tz])
        rpb = self.iw["na_rpb"]
        t_rb = P.dma("sp", self.RB[:, :, 48:79], rpb, self.st_ds, deps=[t_rb], slow=True)
        self.iota_free(t1, 64, ti)
        t_k = self.cp("pool", kv, self.pidx[:], deps=[self.t_const])
        t_k = P.op("pool", lambda g: g.tensor_scalar(out=kv[64:128, :], in0=kv[64:128, :], scalar1=-64.0, scalar2=None, op0=ALU.add))
        self.ts("dve", t1, t1, -8.0, ALU.add, 0.0, ALU.max, deps=[t_k])
        self.ts("dve", t1, t1, 48.0, ALU.min)
        self.ts("dve", t1, t1, kv[:, 0:1], ALU.subtract, -1.0, ALU.mult)
        self.ts("dve", t2, t1, 0.0, ALU.is_ge)
        self.ts("dve", t1, t1, 16.0, ALU.is_lt)
        self.tt("dve", t1, t1, t2, ALU.mult)
        t_mk = self.ts("dve", mk, t1, -1.0, ALU.add, 30000.0, ALU.mult)
        P.barrier()
        wq = self.iw["na_w_qkv"].rearrange("(kc p) (three h d) -> p kc three h d", p=128, three=3, h=16)
        for s in range(self.nseq):
            self.sq_free = self.rstd_free = None
            for tb in range(S // NT):
                t_x = self.load_x_block(s, tb * NT, deps=[self.xa_free])
                t_h = self.rmsnorm_block(t_x, 2, hT[:, :, tb * NT:(tb + 1) * NT])
                self.xa_free = t_h
            P.barrier()
            import os as _os
            for hd in range(int(_os.environ.get('NA_HEADS', '16'))):
                for wh in range(3):
                    t_w = P.dma("pool", wq3[:, :, wh, :], wq[:, :, wh, hd, :], self.ld_ds)
                for hf_ in range(2):
                    t_tp = P.dma("sp", toep_tmp[hf_ * 64:(hf_ + 1) * 64], bass.AP(tensor=self.RB.tensor, offset=self.RB.offset + hd * 15 * 128, ap=[[1, 64], [128, 15], [1, 64]]),
                                 self.c_ds, deps=[t_rb])
                t_tp = self.tt("dve", toep, toep_tmp[:, :, ::-1], mk.unsqueeze(1).to_broadcast([128, 15, 64]), ALU.add, deps=[t_tp, t_mk])
                for which, dst, gain in ((0, qT, gq), (1, kT, gk)):
                    for tb in range(4):
                        b, pb = self.bank()
                        for k in range(KC):
                            tk = self.mm(pb, wq3[:, k, which, :], hT[:, k, tb * NT:(tb + 1) * NT], k == 0, k == KC - 1, deps=[t_w, self.bank_free[b]], signal=(k == KC - 1))
                        t_s = self.act(sqh, pb, AF.Square, deps=[tk])
                        b2, pb2 = self.bank()
                        tk2 = self.mm(pb2, self.ones[:], sqh, True, True, deps=[t_s, self.bank_free[b2]], signal=True)
                        t_r = self.act(rsq, pb2, AF.Sqrt, bias=self.epsc[:, 0:1], scale=1.0 / 128, deps=[tk2, getattr(self, "naq_free", None)])
                        self.bank_free[b2] = t_r
                        P.op("dve", lambda v: v.reciprocal(out=rsq, in_=rsq), deps=[t_r])
                        t_q = self.stt("dve", dst[:, tb * NT:(tb + 1) * NT], pb, gain[:, 0:1], rsq, ALU.mult, ALU.mult, deps=[t_gq])
                        self.bank_free[b] = t_q
                        self.naq_free = t_q
                for R4 in range(8):
                    b, pb = self.bank()
                    for j in range(4):
                        R = R4 * 4 + j
                        for k in range(KC):
                            tk = self.mm(pb[0:64, j * 128:(j + 1) * 128], hT[:, k, R * 64:(R + 1) * 64], wq3[:, k, 2, :], k == 0, k == KC - 1,
                                         deps=[t_w, self.bank_free[b]], signal=(j == 3 and k == KC - 1))
                    t_v = self.cp("act", vrow[0:64, R4 * 4:(R4 + 1) * 4, :], pb[0:64, :].rearrange("p (a b) -> p a b", a=4), deps=[tk])
                    self.bank_free[b] = t_v
                t_qkv = Tok(P.eng["dve"].sem, P.eng["dve"].cnt)
                t_o = None
                import os as _os
                for r in range(int(_os.environ.get('NA_ROWS', '32'))):
                    r0 = min(max(r - 4, 0), 24)
                    v_ = r - r0
                    d0 = 7 - v_
                    b, pb = self.bank()
                    for i in range(8):
                        R = r0 + i
                        tk = self.mm(pb[0:64, i * 64:(i + 1) * 64], kT[:, R * 64:(R + 1) * 64], qT[:, r * 64:(r + 1) * 64], True, True,
                                     deps=[t_qkv, t_v, self.bank_free[b]], signal=(i == 7))
                    t_a = self.tt("dve", sbuf_[0:64], pb[0:64, :].rearrange("p (a b) -> p a b", a=8), toep[0:64, d0:d0 + 8, :], ALU.add, deps=[tk, t_tp])
                    self.bank_free[b] = t_a
                    t_e = self.act(ebuf[0:64], sbuf_[0:64], AF.Exp, deps=[t_a, t_o])
                    b2, pb2 = self.bank()
                    for i in range(8):
                        R = r0 + i
                        self.mm(pb2[:, 0:64], vrow[0:64, R, :], ebuf[0:64, i, :], i == 0, i == 7, deps=[t_e, self.bank_free[b2]])
                    for i in range(8):
                        tk = self.mm(pb2[:, 64:128], self.ones[0:64, :], ebuf[0:64, i, :], i == 0, i == 7, signal=(i == 7))
                    t_o = P.op("dve", lambda v, pb2=pb2: v.reciprocal(out=rcp, in_=pb2[:, 64:128]), deps=[tk])
                    t_o = self.tt("dve", oT[:, r * 64:(r + 1) * 64], pb2[:, 0:64], rcp, ALU.mult)
                    self.bank_free[b2] = t_o
                t_st = P.dma("sp", self.HT[s, hd * 128:(hd + 1) * 128, :], oT, self.st_ds, deps=[t_o])
                P.barrier()
            hb_free = None
            for tb in range(S // NT):
                t_x = self.load_x_block(s, tb * NT, deps=[self.xa_free])
                t_z = P.dma("sp", self.hB[:, :, 0:NT], self.xview(self.HT, s, tb * NT), self.ld_ds, deps=[hb_free])
                st = {}

                def ep(m, pb, tk):
                    st["e"] = self.tt("dve", self.xA[:, m, :], pb, self.xA[:, m, :], ALU.add, deps=[tk, t_x], serial=False)
                    return st["e"]
                hb_free = self.linear_block(self.iw["na_w_o"], D, 0, self.hB[:, :, 0:NT], t_z, ep)
                self.xa_free = self.store_x_block(s, tb * NT, deps=[st["e"]])
            P.barrier()
        self.sq_free = self.rstd_free = self.hb_free = self.c_free = None

    def s5(self):
        P = self.P
        M = self.MISC
        SM = 32768

        def small(i):
            return self.cv(SM + 512 * i, [128], F32)
        are, aim, ldt, lre, dtv, mag, thx, cs, sn, abr, abi, den, fr, fi, ta, tb_ = [small(i) for i in range(16)]
        rho = [self.cv(M + 512 * d, [128], F32) for d in range(2)]
        thn = [self.cv(M + 1024 + 512 * d, [128], F32) for d in range(2)]
        dvec = self.cv(M + 2048, [KC], F32)
        msk = self.cv(M + 2112, [8, 128], F32)
        iof = self.cv(M + 6208, [2048], F32)
        tmpi = self.cv(M + 14400, [1024], I32)
        Cb = 66560
        X1 = [self.cv(Cb + 16384 * d, [2048], F32) for d in range(2)]
        X2 = [self.cv(Cb + 16384 * d + 8192, [2048], F32) for d in range(2)]
        W1 = [self.cv(Cb + 32768 + 16384 * d, [2048], F32) for d in range(2)]
        W2 = [self.cv(Cb + 32768 + 16384 * d + 8192, [2048], F32) for d in range(2)]
        Ust = self.C[:, 0:32, :].bitcast(F32).rearrange("p a b -> p (a b)").rearrange("p (m t) -> p m t", m=16)
        u_free = None
        for s in range(self.nseq):
            for tb in range(S // NT):
                t_x = self.load_x_block(s, tb * NT, deps=[self.xa_free])
                t_h = self.rmsnorm_block(t_x, 3, self.hB[:, :, 0:NT], deps_out=[self.hb_free])
                self.xa_free = t_h
                st = {}

                def ep(m, pb, tk):
                    st["u"] = self.cp("act", Ust[:, m, :], pb, deps=[tk, u_free])
                    return st["u"]
                self.hb_free = self.linear_block(self.iw["s5_w_in"], D, 0, self.hB[:, :, 0:NT], t_h, ep)
                u_free = P.dma("sp", self.xview(self.UT, s, tb * NT), Ust, self.st_ds, deps=[st["u"]])
        P.barrier()
        Bre = self.cv(0, [128, 16], F32)
        Bim = self.cv(8192, [128, 16], F32)
        T1 = self.cv(16384, [128, 16], F32)
        T2 = self.cv(24576, [128, 16], F32)
        cn = self.cv(164864, [2, 64], F32)
        t_io = self.iota_free(iof, 2048, self.cv(0, [2048], I32))
        mi = self.cv(8192, [8, 128], I32)
        mt_ = self.cv(16384, [8, 128], F32)
        P.op("pool", lambda g: g.iota(mi, pattern=[[-16, 8], [1, 128]], base=0, channel_multiplier=0))
        t_mi = self.cp("pool", msk, mi)
        self.ts("dve", mt_, msk, 0.0, ALU.is_ge, deps=[t_mi])
        self.ts("dve", msk, msk, 16.0, ALU.is_lt)
        t_msk = self.tt("dve", msk, msk, mt_, ALU.mult)
        t_d = P.dma("sp", dvec, self.iw["s5_d"].rearrange("(t g) c -> (g c) t", g=8), self.c_ds, slow=True)
        P.barrier()
        for d in range(2):
            toks = []
            for hf in range(2):
                toks.append(P.dma("sp", are[hf * 64:(hf + 1) * 64, :], self.iw["s5_a_re"][d].rearrange("g p -> p g"), self.c_ds, slow=True))
                toks.append(P.dma("sp", aim[hf * 64:(hf + 1) * 64, :], self.iw["s5_a_im"][d].rearrange("g p -> p g"), self.c_ds, slow=True))
                toks.append(P.dma("sp", Bre[hf * 64:(hf + 1) * 64], self.iw["s5_b_re"][d].rearrange("g p c -> p g c"), self.c_ds))
                toks.append(P.dma("sp", Bim[hf * 64:(hf + 1) * 64], self.iw["s5_b_im"][d].rearrange("g p c -> p g c"), self.c_ds))
            ld = self.iw["s5_log_dt"]
            toks.append(P.dma("sp", ldt, bass.AP(tensor=ld.tensor, offset=ld.offset + d * 128, ap=[[0, 128], [1, 128]]), self.c_ds, slow=True))
            self.ts("dve", lre, are, -1e-4, ALU.min, deps=toks)
            t = self.act(dtv, ldt, AF.Exp, deps=toks)
            self.tt("dve", ta, lre, dtv, ALU.mult, deps=[t])
            t = self.act(mag, ta, AF.Exp, deps=[Tok(P.eng["dve"].sem, P.eng["dve"].cnt)])
            self.cp("dve", rho[d], mag, deps=[t])
            self.tt("dve", thx, aim, dtv, ALU.mult)
            self.ts("dve", thn[d], thx, 1.0 / TWO_PI, ALU.mult)
            t = self.trig(sn, thn[d], 1.0, 1.0, 0.0, ta, tb_)
            t = self.trig(cs, thn[d], 1.0, 1.0, 0.25, ta, tb_, deps=[t])
            self.tt("dve", abr, mag, cs, ALU.mult, deps=[t])
            self.tt("dve", abi, mag, sn, ALU.mult)
            self.tt("dve", ta, lre, lre, ALU.mult)
            self.tt("dve", tb_, aim, aim, ALU.mult)
            self.tt("dve", den, ta, tb_, ALU.add)
            P.op("dve", lambda v: v.reciprocal(out=den, in_=den))
            self.ts("dve", abr, abr, -1.0, ALU.add)
            self.tt("dve", ta, abr, lre, ALU.mult)
            self.tt("dve", tb_, abi, aim, ALU.mult)
            self.tt("dve", ta, ta, tb_, ALU.add)
            self.tt("dve", fr, ta, den, ALU.mult)
            self.tt("dve", ta, abi, lre, ALU.mult)
            self.tt("dve", tb_, abr, aim, ALU.mult)
            self.tt("dve", ta, ta, tb_, ALU.subtract)
            self.tt("dve", fi, ta, den, ALU.mult)
            frb = fr.unsqueeze(2).to_broadcast([128, 128, 16])
            fib = fi.unsqueeze(2).to_broadcast([128, 128, 16])
            x1v = X1[d].rearrange("p (g c) -> p g c", c=16)
            x2v = X2[d].rearrange("p (g c) -> p g c", c=16)
            self.tt("dve", T1, Bre, frb, ALU.mult)
            self.tt("dve", T2, Bim, fib, ALU.mult)
            self.tt("dve", T1, T1, T2, ALU.subtract)
            self.tt("dve", T2, Bim, frb, ALU.mult)
            self.tt("dve", Bim, Bre, fib, ALU.mult)
            self.tt("dve", T2, T2, Bim, ALU.add)
            self.cp("dve", x1v[0:64], T1[0:64])
            self.cp("dve", x1v[64:128], T2[64:128])
            self.cp("dve", x2v[0:64], T2[0:64])
            self.ts("dve", x2v[64:128], T1[64:128], -1.0, ALU.mult)
            for T in range(16):
                for ri, nm in enumerate(("s5_c_re", "s5_c_im")):
                    srcn = self.iw[nm][d, T * 8:(T + 1) * 8].rearrange("g c p -> (g c) p")
                    P.wait("sp", [Tok(P.eng["pe"].sem, P.eng["pe"].cnt)])
                    t0_ = P.dma("sp", cn[:, 0, :], srcn, self.ld_ds)
                    t1_ = P.dma("sp", cn[:, 1, :], srcn, self.ld_ds)
                    b, pb = self.bank()
                    tk = P.op("pe", lambda t, pb=pb: t.transpose(pb[:, 0:128], cn.rearrange("p a b -> p (a b)"), self.identf[:]),
                              deps=[t0_, t1_, self.bank_free[b], self.t_const], signal=True)
                    w1v = W1[d][:, T * 128:(T + 1) * 128]
                    w2v = W2[d][:, T * 128:(T + 1) * 128]
                    if ri == 0:
                        self.cp("dve", w1v[0:64], pb[0:64, 0:128], deps=[tk])
                        te = self.ts("dve", w2v[64:128], pb[64:128, 0:128], -1.0, ALU.mult)
                    else:
                        self.ts("dve", w1v[64:128], pb[64:128, 0:128], -1.0, ALU.mult, deps=[tk])
                        te = self.ts("dve", w2v[0:64], pb[0:64, 0:128], -1.0, ALU.mult)
                    self.bank_free[b] = te
            P.barrier()
        cosT = self.cv(0, [2048], F32)
        sinT = self.cv(8192, [2048], F32)
        St = self.cv(16384, [2048], F32)
        Xt = self.cv(24576, [2048], F32)
        Pc = self.cv(32768, [2048], BF16)
        Ps = self.cv(36864, [2048], BF16)
        ub = self.cv(40960, [2048], BF16)
        uf = self.cv(45056, [2048], F32)
        yt = self.cv(53248, [2048], F32)
        yg = self.cv(61440, [2048], BF16)
        Wb = 132096
        LB1 = [self.cv(Wb + 4096 * d, [8, 128], BF16) for d in range(2)]
        LB2 = [self.cv(Wb + 4096 * d + 2048, [8, 128], BF16) for d in range(2)]
        WP1 = [self.cv(Wb + 8192 + 4096 * d, [8, 128], BF16) for d in range(2)]
        WP2 = [self.cv(Wb + 8192 + 4096 * d + 2048, [8, 128], BF16) for d in range(2)]
        tmpm = self.cv(Wb + 16384, [8, 128], F32)
        tmp1 = self.cv(Wb + 20480, [2048], F32)
        tmp2 = self.cv(164864, [2048], F32)
        for T in range(16):
            for d in range(2):
                for (src, dst) in ((X1[d], LB1[d]), (X2[d], LB2[d])):
                    t_m = self.tt("dve", tmpm, src[:, T * 128:(T + 1) * 128].unsqueeze(1).to_broadcast([128, 8, 128]), msk, ALU.mult, deps=[t_msk])
                    for h4 in range(2):
                        b, pb = self.bank()
                        for j in range(4):
                            tk = P.op("pe", lambda t, o=pb[:, j * 128:(j + 1) * 128], i=tmpm[:, h4 * 4 + j, :]: t.transpose(o, i, self.identf[:]),
                                      deps=[t_m, self.bank_free[b]], signal=(j == 3))
                        te = self.cp("act", dst[:, h4 * 4:(h4 + 1) * 4, :], pb.rearrange("p (a b) -> p a b", a=4), deps=[tk])
                        self.bank_free[b] = te
                    P.wait("dve", [tk])
                self.tt("dve", WP1[d], W1[d][:, T * 128:(T + 1) * 128].unsqueeze(1).to_broadcast([128, 8, 128]), msk, ALU.mult)
                self.tt("dve", WP2[d], W2[d][:, T * 128:(T + 1) * 128].unsqueeze(1).to_broadcast([128, 8, 128]), msk, ALU.mult)
            P.barrier()
            for s in range(self.nseq):
                t_u = P.dma("sp", uf, self.UT[s, T * 128:(T + 1) * 128, :], self.ld_ds)
                t_ub = self.cp("act", ub, uf, deps=[t_u])
                ybanks = [self.ps[:, i * 512:(i + 1) * 512] for i in range(4)]
                sb_i = 0
                pc_free = None
                for d in range(2):
                    for gl in range(8):
                        g = T * 8 + gl
                        first = (d == 0 and gl == 0)
                        last = (d == 1 and gl == 7)
                        t_tab = self.trig2(cosT, sinT, iof, thn[d][:, g:g + 1], tmp1, tmp2)
                        for tb in range(4):
                            b1 = 4 + sb_i
                            b2 = 5 + sb_i
                            sb_i = 2 - sb_i
                            pb1 = self.ps[:, b1 * 512:(b1 + 1) * 512]
                            pb2 = self.ps[:, b2 * 512:(b2 + 1) * 512]
                            self.mm(pb1, LB1[d][:, gl, :], ub[:, tb * NT:(tb + 1) * NT], True, True, deps=[t_ub, self.bank_free[b1]])
                            tk = self.mm(pb2, LB2[d][:, gl, :], ub[:, tb * NT:(tb + 1) * NT], True, True, deps=[self.bank_free[b2]], signal=True)
                            if d == 0:
                                sl = slice(tb * NT, (tb + 1) * NT)
                                i1, i2 = pb1, pb2
                            else:
                                sl = slice(S - (tb + 1) * NT, S - tb * NT)
                                i1, i2 = pb1[:, ::-1], pb2[:, ::-1]
                            self.tt("dve", St[:, sl], i1, cosT[:, sl], ALU.mult, deps=[tk, t_tab])
                            t_1 = self.tt("dve", tmp1[:, 0:NT], i2, sinT[:, sl], ALU.mult)
                            self.bank_free[b1] = t_1
                            self.bank_free[b2] = t_1
                            self.tt("dve", St[:, sl], St[:, sl], tmp1[:, 0:NT], ALU.add)
                        P.op("dve", lambda v, d=d, g=g: v.tensor_tensor_scan(out=Xt, data0=rho[d][:, g:g + 1].to_broadcast([128, S]), data1=St, initial=0.0,
                                                                           op0=ALU.mult, op1=ALU.add))
                        po = (lambda a: a) if d == 0 else (lambda a: a[:, ::-1])
                        self.tt("dve", po(Pc), Xt, cosT, ALU.mult, deps=[pc_free])
                        t_p = self.tt("dve", po(Ps), Xt, sinT, ALU.mult)
                        for tb in range(4):
                            self.mm(ybanks[tb], WP1[d][:, gl, :], Pc[:, tb * NT:(tb + 1) * NT], first, False, deps=[t_p])
                            pc_free = self.mm(ybanks[tb], WP2[d][:, gl, :], Ps[:, tb * NT:(tb + 1) * NT], False, last, signal=True)
                t_y = None
                for tb in range(4):
                    t_y = self.stt("dve", yt[:, tb * NT:(tb + 1) * NT], uf[:, tb * NT:(tb + 1) * NT], dvec[:, T:T + 1], ybanks[tb], ALU.mult, ALU.add, deps=[pc_free, t_d])
                t_g = self.act(yg, yt, AF.Gelu, deps=[t_y])
                P.dma("sp", self.HT[s, T * 128:(T + 1) * 128, :], yg, self.st_ds, deps=[t_g])
                P.barrier()
        self.bank_free = [None] * 8
        hb_free = None
        w_glu = self.iw["s5_w_glu"]
        self.tmp_free = [None] * 4
        for s in range(self.nseq):
            for tb in range(S // NT):
                t_x = self.load_x_block(s, tb * NT, deps=[self.xa_free])
                t_z = P.dma("sp", self.hB[:, :, 0:NT], self.xview(self.HT, s, tb * NT), self.ld_ds, deps=[hb_free])
                st = {}

                def ep_g(m, pb, tk):
                    j = m % 4
                    st[m] = self.act(self.tmpf[j][:], pb, AF.Sigmoid, deps=[tk, self.tmp_free[j]])
                    return st[m]

                def ep_a(m, pb, tk):
                    j = m % 4
                    t = self.tt("dve", self.tmpf[j][:], pb, self.tmpf[j][:], ALU.mult, deps=[tk, st[m]])
                    st["e"] = self.tt("dve", self.xA[:, m, :], self.xA[:, m, :], self.tmpf[j][:], ALU.add, deps=[t_x])
                    self.tmp_free[j] = st["e"]
                    return t
                for mg in range(4):
                    self.linear_block(w_glu, 512, D + mg * 512, self.hB[:, :, 0:NT], t_z, lambda m, pb, tk, mg=mg: ep_g(mg * 4 + m, pb, tk))
                    hb_free = self.linear_block(w_glu, 512, mg * 512, self.hB[:, :, 0:NT], t_z, lambda m, pb, tk, mg=mg: ep_a(mg * 4 + m, pb, tk))
                self.xa_free = self.store_x_block(s, tb * NT, deps=[st["e"]])
        P.barrier()
        self.sq_free = self.rstd_free = self.hb_free = self.c_free = None
        self.tmp_free = [None] * 4

    def build(self):
        P = self.P
        self.sq_free = None
        self.rstd_free = None
        self.xa_free = None
        self.hb_free = None
        self.c_free = None
        self.tmp_free = [None] * 4
        self.setup_consts()
        self.transpose_in()
        self.xa_free = self.xt_ready
        for si, sl in enumerate(self.sublayers):
            P.barrier()
            if sl.startswith("mlp"):
                self.mlp(int(sl[3:]))
            else:
                nxt = self.sublayers[si + 1] if si + 1 < len(self.sublayers) else ""
                if nxt.startswith("mlp"):
                    self.preconvert(int(nxt[3:]))
                getattr(self, sl)()
        P.barrier()
        self.transpose_out()
        P.wait("sp", [self.final_tok])
        P.emit()
        return self.nc


FULL = ["fnet", "mlp0", "conv", "mlp1", "na", "mlp2", "s5", "mlp3"]


def run(inputs, nseq=2, sublayers=None, ncores=NCORES, debug=False):
    if sublayers is None:
        sublayers = FULL
    kb = K(nseq, sublayers, debug)
    with kb.stack:
        nc = kb.build()
    host = {}
    for k, v in inputs.items():
        v = np.asarray(v)
        if k in ("mlp_w1", "mlp_w2"):
            for i in range(4):
                host["%s_%d" % (k, i)] = v[i]
        elif k in ("mix_norm", "mlp_norm", "x"):
            host[k] = v
        else:
            host[k] = v[0]
    in_maps = []
    for c in range(ncores):
        m = {"x": np.ascontiguousarray(host["x"][c * nseq:(c + 1) * nseq])}
        for n in kb.in_names:
            m[n] = np.ascontiguousarray(host[n])
        in_maps.append(m)
    res = run_bass_kernel_spmd(nc, in_maps, core_ids=list(range(ncores)))
    if debug:
        return res.results
    return np.concatenate([r["out"] for r in res.results], axis=0)


def kernel(**inputs):
    return run(inputs, nseq=2, sublayers=None).astype(np.float32)
```
